# Optimizing a Trainium2 kernel written in Bass

```python
import math
import jax, jax.numpy as jnp
from jax import lax
import numpy as np

D_MODEL = 4096
BATCH = 4
SEQ = 4096
DEPTH = 2

MEM_LEN = 256
MIX_W = D_MODEL
ATTN_HEAD_DIM = 128
ATTN_W = MIX_W // 2
N_ATTN_HEADS = ATTN_W // ATTN_HEAD_DIM
CONV_W = MIX_W // 4
SSM_W = MIX_W - ATTN_W - CONV_W
CONV_WIDTH = 31
SSM_GROUP_CH = 16
SSM_GROUPS = SSM_W // SSM_GROUP_CH
SSM_STATE = 64
DILATION_PATTERNS = ((128, 1), (512, 4), (2048, 16))
N_MEM_HEADS = 4
MEM_HEAD_DIM = 128
MEM_W = N_MEM_HEADS * MEM_HEAD_DIM
D_FF = 4 * D_MODEL
IN_W = 3 * ATTN_W + 2 * CONV_W + SSM_W
EPS = 1e-6
NEG_INF = -1e30

kernel_name = "hybrid_dilattn_conformer_s5_block"


def rms_norm(x, g):
    xf = x.astype(jnp.float32)
    y = xf * lax.rsqrt(jnp.mean(xf * xf, axis=-1, keepdims=True) + EPS)
    return (y * g.astype(jnp.float32)).astype(x.dtype)


def layer_norm(x, g, b):
    xf = x.astype(jnp.float32)
    mu = jnp.mean(xf, axis=-1, keepdims=True)
    xc = xf - mu
    y = xc * lax.rsqrt(jnp.mean(xc * xc, axis=-1, keepdims=True) + EPS)
    return (y * g.astype(jnp.float32) + b.astype(jnp.float32)).astype(x.dtype)


def dilated_window_attention(q, k, v, window, dilation):
    bsz, seq, heads, hd = q.shape
    blk = window // dilation
    sub_len = seq // dilation
    n_blk = -(-sub_len // blk)
    pad_len = n_blk * blk - sub_len

    def to_blocks(t):
        t = t.reshape(bsz, sub_len, dilation, heads, hd).transpose(0, 2, 3, 1, 4)
        t = jnp.pad(t, ((0, 0), (0, 0), (0, 0), (0, pad_len), (0, 0)))
        return t.reshape(bsz, dilation, heads, n_blk, blk, hd)

    def with_prev(t):
        prev = jnp.pad(t, ((0, 0), (0, 0), (0, 0), (1, 0), (0, 0), (0, 0)))[:, :, :, :-1]
        return jnp.concatenate([prev, t], axis=-2)

    qb = to_blocks(q)
    kb = with_prev(to_blocks(k))
    vb = with_prev(to_blocks(v))
    s = jnp.einsum('brhnqe,brhnke->brhnqk', qb, kb)
    qi = jnp.arange(blk)[:, None]
    kj = jnp.arange(2 * blk)[None, :]
    dist = qi + blk - kj
    bi = jnp.arange(n_blk)[:, None, None]
    valid = (dist >= 0) & (dist <= blk) & (bi * blk + kj - blk >= 0)
    s = jnp.where(valid, s, NEG_INF)
    m = jnp.max(s, axis=-1, keepdims=True)
    p = jnp.exp(s - m)
    den = jnp.sum(p, axis=-1, keepdims=True)
    o = jnp.einsum('brhnqk,brhnke->brhnqe', p, vb) / den
    lse = (m + jnp.log(den))[..., 0]
    o = o.reshape(bsz, dilation, heads, n_blk * blk, hd)[:, :, :, :sub_len]
    o = o.transpose(0, 3, 1, 2, 4).reshape(bsz, seq, heads, hd)
    lse = lse.reshape(bsz, dilation, heads, n_blk * blk)[..., :sub_len]
    lse = lse.transpose(0, 3, 1, 2).reshape(bsz, seq, heads)
    return o, lse


def dilated_mixture_attention(q, k, v, q_g, k_g):
    bsz, seq, _ = q.shape
    shp = (bsz, seq, N_ATTN_HEADS, ATTN_HEAD_DIM)
    qf = rms_norm(q.reshape(shp).astype(jnp.float32), q_g) * (ATTN_HEAD_DIM ** -0.5)
    kf = rms_norm(k.reshape(shp).astype(jnp.float32), k_g)
    vf = v.reshape(shp).astype(jnp.float32)
    outs, lses = [], []
    for window, dilation in DILATION_PATTERNS:
        o, lse = dilated_window_attention(qf, kf, vf, window, dilation)
        outs.append(o)
        lses.append(lse)
    wts = jax.nn.softmax(jnp.stack(lses, axis=0), axis=0)
    o = jnp.sum(wts[..., None] * jnp.stack(outs, axis=0), axis=0)
    return o.reshape(bsz, seq, ATTN_W).astype(q.dtype)


def conformer_conv(a, gate, dw, dw_b, ln_g, ln_b):
    h = a * jax.nn.sigmoid(gate)
    h = lax.conv_general_dilated(h, dw.astype(h.dtype), window_strides=(1,),
                                 padding=[(CONV_WIDTH - 1, 0)],
                                 dimension_numbers=('NWC', 'WIO', 'NWC'),
                                 feature_group_count=CONV_W)
    h = h + dw_b.astype(h.dtype)
    return jax.nn.silu(layer_norm(h, ln_g, ln_b))


def s5_glu(u, a_re, a_im, b_re, b_im, c_re, c_im, d_skip, log_step, w_glu, b_glu):
    bsz, seq, _ = u.shape
    f32 = jnp.float32
    uf = u.astype(f32).reshape(bsz, seq, SSM_GROUPS, SSM_GROUP_CH)
    lam = lax.complex(a_re.astype(f32), a_im.astype(f32))
    step = jnp.exp(log_step.astype(f32))[:, None]
    lam_bar = jnp.exp(lam * step)
    b = lax.complex(b_re.astype(f32), b_im.astype(f32))
    b_bar = ((lam_bar - 1.0) / lam)[..., None] * b
    bu = jnp.einsum('bsgh,gph->bsgp', uf.astype(jnp.complex64), b_bar)
    a_seq = jnp.broadcast_to(lam_bar, bu.shape)

    def combine(left, right):
        a_l, x_l = left
        a_r, x_r = right
        return a_r * a_l, a_r * x_l + x_r

    _, states = lax.associative_scan(combine, (a_seq, bu), axis=1)
    c = lax.complex(c_re.astype(f32), c_im.astype(f32))
    y = jnp.einsum('bsgp,ghp->bsgh', states, c).real
    y = y + d_skip.astype(f32).reshape(SSM_GROUPS, SSM_GROUP_CH) * uf
    z = jax.nn.gelu(y.reshape(bsz, seq, SSM_W))
    out = z * jax.nn.sigmoid(z @ w_glu.astype(f32) + b_glu.astype(f32))
    return out.astype(u.dtype)


def memory_cross_attention(h, mem_n, w_cq, w_ckv, cq_g, ck_g, w_co):
    bsz, seq, _ = h.shape
    q = (h @ w_cq).reshape(bsz, seq, N_MEM_HEADS, MEM_HEAD_DIM).astype(jnp.float32)
    kv = mem_n @ w_ckv
    k, v = jnp.split(kv, 2, axis=-1)
    mlen = mem_n.shape[1]
    k = k.reshape(bsz, mlen, N_MEM_HEADS, MEM_HEAD_DIM).astype(jnp.float32)
    v = v.reshape(bsz, mlen, N_MEM_HEADS, MEM_HEAD_DIM).astype(jnp.float32)
    q = rms_norm(q, cq_g) * (MEM_HEAD_DIM ** -0.5)
    k = rms_norm(k, ck_g)
    p = jax.nn.softmax(jnp.einsum('bshe,bmhe->bhsm', q, k), axis=-1)
    o = jnp.einsum('bhsm,bmhe->bshe', p, v).reshape(bsz, seq, MEM_W).astype(h.dtype)
    return o @ w_co


def setup_inputs(seed: int = 0) -> dict:
    key = jax.random.key(seed)
    ks = jax.random.split(key, 40)
    f32 = jnp.float32

    def nrm(k, shape, scale):
        return jax.random.normal(k, shape, f32) * scale

    def gain(k, shape):
        return 1.0 + 0.01 * jax.random.normal(k, shape, f32)

    L = DEPTH
    a_im = jnp.pi * jnp.broadcast_to(jnp.arange(SSM_STATE, dtype=f32), (L, SSM_GROUPS, SSM_STATE))
    return {
        "x": jax.random.normal(ks[0], (BATCH, SEQ, D_MODEL), f32),
        "mem": jax.random.normal(ks[1], (BATCH, MEM_LEN, D_MODEL), f32),
        "norm_mix": gain(ks[2], (L, D_MODEL)),
        "w_in": nrm(ks[3], (L, D_MODEL, IN_W), D_MODEL ** -0.5),
        "q_norm": gain(ks[4], (L, ATTN_HEAD_DIM)),
        "k_norm": gain(ks[5], (L, ATTN_HEAD_DIM)),
        "conv_dw": nrm(ks[6], (L, CONV_WIDTH, 1, CONV_W), CONV_WIDTH ** -0.5),
        "conv_b": nrm(ks[7], (L, CONV_W), 0.01),
        "conv_ln_g": gain(ks[8], (L, CONV_W)),
        "conv_ln_b": nrm(ks[9], (L, CONV_W), 0.01),
        "ssm_a_re": -0.5 + nrm(ks[10], (L, SSM_GROUPS, SSM_STATE), 0.01),
        "ssm_a_im": a_im + nrm(ks[11], (L, SSM_GROUPS, SSM_STATE), 0.01),
        "ssm_b_re": nrm(ks[12], (L, SSM_GROUPS, SSM_STATE, SSM_GROUP_CH), (2 * SSM_GROUP_CH) ** -0.5),
        "ssm_b_im": nrm(ks[13], (L, SSM_GROUPS, SSM_STATE, SSM_GROUP_CH), (2 * SSM_GROUP_CH) ** -0.5),
        "ssm_c_re": nrm(ks[14], (L, SSM_GROUPS, SSM_GROUP_CH, SSM_STATE), (2 * SSM_STATE) ** -0.5),
        "ssm_c_im": nrm(ks[15], (L, SSM_GROUPS, SSM_GROUP_CH, SSM_STATE), (2 * SSM_STATE) ** -0.5),
        "ssm_d": nrm(ks[16], (L, SSM_W), 1.0),
        "ssm_log_step": jax.random.uniform(ks[17], (L, SSM_GROUPS), f32,
                                           minval=math.log(1e-3), maxval=math.log(1e-1)),
        "ssm_w_glu": nrm(ks[18], (L, SSM_W, SSM_W), SSM_W ** -0.5),
        "ssm_b_glu": nrm(ks[19], (L, SSM_W), 0.01),
        "mix_out_norm": gain(ks[20], (L, MIX_W)),
        "w_out": nrm(ks[21], (L, MIX_W, D_MODEL), MIX_W ** -0.5),
        "norm_cross": gain(ks[22], (L, D_MODEL)),
        "norm_mem": gain(ks[23], (L, D_MODEL)),
        "w_cq": nrm(ks[24], (L, D_MODEL, MEM_W), D_MODEL ** -0.5),
        "w_ckv": nrm(ks[25], (L, D_MODEL, 2 * MEM_W), D_MODEL ** -0.5),
        "cq_norm": gain(ks[26], (L, MEM_HEAD_DIM)),
        "ck_norm": gain(ks[27], (L, MEM_HEAD_DIM)),
        "w_co": nrm(ks[28], (L, MEM_W, D_MODEL), MEM_W ** -0.5),
        "norm_mlp": gain(ks[29], (L, D_MODEL)),
        "w_up": nrm(ks[30], (L, D_MODEL, D_FF), D_MODEL ** -0.5),
        "w_down": nrm(ks[31], (L, D_FF, D_MODEL), D_FF ** -0.5),
    }


def reference(x, mem, norm_mix, w_in, q_norm, k_norm, conv_dw, conv_b, conv_ln_g, conv_ln_b,
              ssm_a_re, ssm_a_im, ssm_b_re, ssm_b_im, ssm_c_re, ssm_c_im, ssm_d, ssm_log_step,
              ssm_w_glu, ssm_b_glu, mix_out_norm, w_out, norm_cross, norm_mem, w_cq, w_ckv,
              cq_norm, ck_norm, w_co, norm_mlp, w_up, w_down):
    split_at = [ATTN_W, 2 * ATTN_W, 3 * ATTN_W, 3 * ATTN_W + CONV_W, 3 * ATTN_W + 2 * CONV_W]
    for l in range(DEPTH):
        h = rms_norm(x, norm_mix[l])
        proj = h @ w_in[l]
        q, k, v, conv_a, conv_g, ssm_u = jnp.split(proj, split_at, axis=-1)
        attn = dilated_mixture_attention(q, k, v, q_norm[l], k_norm[l])
        conv = conformer_conv(conv_a, conv_g, conv_dw[l], conv_b[l], conv_ln_g[l], conv_ln_b[l])
        ssm = s5_glu(ssm_u, ssm_a_re[l], ssm_a_im[l], ssm_b_re[l], ssm_b_im[l], ssm_c_re[l],
                     ssm_c_im[l], ssm_d[l], ssm_log_step[l], ssm_w_glu[l], ssm_b_glu[l])
        g = mix_out_norm[l]
        mixed = jnp.concatenate([
            rms_norm(attn, g[:ATTN_W]),
            rms_norm(conv, g[ATTN_W:ATTN_W + CONV_W]),
            rms_norm(ssm, g[ATTN_W + CONV_W:]),
        ], axis=-1)
        x = x + mixed @ w_out[l]
        x = x + memory_cross_attention(rms_norm(x, norm_cross[l]), rms_norm(mem, norm_mem[l]),
                                       w_cq[l], w_ckv[l], cq_norm[l], ck_norm[l], w_co[l])
        h = rms_norm(x, norm_mlp[l])
        x = x + jnp.square(jax.nn.relu(h @ w_up[l])) @ w_down[l]
    return x
```

```python
import math
import numpy as np
import concourse.bass as bass
import concourse.mybir as mybir
from concourse.bass_utils import run_bass_kernel_spmd

F32 = mybir.dt.float32
BF16 = mybir.dt.bfloat16
I32 = mybir.dt.int32
AF = mybir.ActivationFunctionType
ALU = mybir.AluOpType
AX = mybir.AxisListType

COMPUTE = ("pe", "act", "dve", "pool")
ENGS = ("pe", "act", "dve", "pool", "sp")
SAME_DIST = 3


def I(meth, *a, **k):
    return lambda e: getattr(e, meth)(*a, **k)


class Tok:
    __slots__ = ("name", "lw", "rd", "sem", "cnt", "id")
    _n = 0

    def __init__(self, name, alias=()):
        self.name = name
        self.lw = {}
        self.rd = {}
        self.sem = None
        self.cnt = 0
        Tok._n += 1
        self.id = Tok._n
        for a in alias:
            for src, dst in ((a.lw, self.lw), (a.rd, self.rd)):
                for k, p in src.items():
                    if k not in dst or dst[k].seq < p.seq:
                        dst[k] = p


class Op:
    __slots__ = ("eng", "fn", "waits", "signal", "clock", "seq", "snap", "dtok", "sigval")


class Prog:
    def __init__(self):
        self.ops = {e: [] for e in ENGS}
        self.know = {e: {} for e in ENGS}
        self.nseq = {e: 0 for e in ENGS}
        self.dma_toks = []
        self.semcnt = {}
        self.lsnap = {e: {} for e in ENGS}

    def op(self, eng, fn, rd=(), wr=(), dma=None, skip=()):
        o = Op()
        o.eng = eng
        o.fn = fn
        o.waits = []
        o.signal = False
        o.dtok = dma
        o.sigval = None
        know = self.know[eng]
        if dma is None:
            self.nseq[eng] += 1
            o.clock = eng
            o.seq = self.nseq[eng]
        else:
            sk = dma.name
            if sk not in self.semcnt:
                self.semcnt[sk] = 0
                self.dma_toks.append(sk)
            self.semcnt[sk] += 1
            o.clock = ("d", sk)
            o.seq = self.semcnt[sk]
        deps = []
        for t in rd:
            deps += t.lw.values()
        for t in wr:
            deps += t.lw.values()
            deps += t.rd.values()
        myseq = self.nseq[eng]
        dirty = False
        for p in deps:
            if p is o or p in skip:
                continue
            c = p.clock
            if c == eng:
                if eng == "pe" or eng == "sp":
                    continue
                if myseq - p.seq > SAME_DIST or know.get(c, 0) >= p.seq:
                    continue
            elif know.get(c, 0) >= p.seq:
                continue
            o.waits.append(p)
            p.signal = True
            dirty = True
            for k, v in p.snap.items():
                if know.get(k, 0) < v:
                    know[k] = v
            if know.get(c, 0) < p.seq:
                know[c] = p.seq
        if dirty:
            self.lsnap[eng] = dict(know)
        o.snap = self.lsnap[eng]
        for t in wr:
            t.lw = {o.clock: o}
            t.rd = {}
        for t in rd:
            t.rd[o.clock] = o
        self.ops[eng].append(o)
        return o

    def emit(self, nc, stack):
        sems = {}
        for e in COMPUTE:
            sems[e] = stack.enter_context(nc.semaphore("s_" + e))
        for i, sk in enumerate(self.dma_toks):
            sems[("d", sk)] = stack.enter_context(nc.semaphore("d%d" % i))
        for e in COMPUTE:
            n = 0
            for o in self.ops[e]:
                if o.dtok is None and o.signal:
                    n += 1
                    o.sigval = n
        block = stack.enter_context(nc.Block())

        def run(eng_name):
            def body(eng):
                for o in self.ops[eng_name]:
                    for p in o.waits:
                        if p.dtok is None:
                            eng.wait_ge(sems[p.clock], p.sigval)
                        else:
                            eng.wait_ge(sems[p.clock], 16 * p.seq)
                    if o.fn is None:
                        continue
                    ins = o.fn(eng)
                    if o.dtok is not None:
                        ins.then_inc(sems[o.clock], 16)
                    elif o.signal:
                        ins.then_inc(sems[o.clock], 1)
            return body

        block.tensor(run("pe"))
        block.scalar(run("act"))
        block.vector(run("dve"))
        block.gpsimd(run("pool"))
        block.sync(run("sp"))


class Cfg:
    def __init__(self, D=4096, NT=4096, depth=2, mem_len=256):
        self.D = D
        self.NT = NT
        self.depth = depth
        self.KC = D // 128
        self.TG = min(1024, NT)
        self.NG = NT // self.TG
        self.AW = D // 2
        self.H = self.AW // 128
        self.CW = D // 4
        self.CCH = self.CW // 128
        self.SW = D // 4
        self.SCH = self.SW // 128
        self.INW = 3 * self.AW + 2 * self.CW + self.SW
        self.FF = 4 * D
        self.ML = mem_len
        self.MW = 512
        self.NTT = NT // 128


PARAM_SPECS = None


def param_shapes(c):
    D = c.D
    return {
        "norm_mix": (D,), "w_in": (D, c.INW), "q_norm": (128,), "k_norm": (128,),
        "conv_dw": (31, c.CW), "conv_b": (c.CW,), "conv_ln_g": (c.CW,), "conv_ln_b": (c.CW,),
        "ssm_a_re": (c.SW // 16, 64), "ssm_a_im": (c.SW // 16, 64),
        "ssm_b_re": (c.SW // 16, 64, 16), "ssm_b_im": (c.SW // 16, 64, 16),
        "ssm_c_re": (c.SW // 16, 16, 64), "ssm_c_im": (c.SW // 16, 16, 64),
        "ssm_d": (c.SW,), "ssm_log_step": (c.SW // 16,), "ssm_w_glu": (c.SW, c.SW), "ssm_b_glu": (c.SW,),
        "mix_out_norm": (D,), "w_out": (D, D), "norm_cross": (D,), "norm_mem": (D,),
        "w_cq": (D, 512), "w_ckv": (D, 1024), "cq_norm": (128,), "ck_norm": (128,), "w_co": (512, D),
        "norm_mlp": (D,), "w_up": (D, c.FF), "w_down": (c.FF, D),
    }


class B:
    pass


def _setup(c, dbg):
    from contextlib import ExitStack
    b = B()
    b.c = c
    b.dbg = dbg
    b.stack = ExitStack()
    nc = bass.Bass("TRN2", target_bir_lowering=False)
    b.nc = nc
    b.P = Prog()
    b.x_in = nc.dram_tensor("x", [c.NT, c.D], F32, kind="ExternalInput").ap()
    b.mem_in = nc.dram_tensor("mem", [c.ML, c.D], F32, kind="ExternalInput").ap()
    b.prm = {}
    for n, s in param_shapes(c).items():
        b.prm[n] = nc.dram_tensor(n, [c.depth] + list(s), F32, kind="ExternalInput").ap()
    b.cst = {}
    for n, s in const_shapes(c).items():
        b.cst[n] = nc.dram_tensor(n, list(s), F32, kind="ExternalInput").ap()
    b.out = nc.dram_tensor("out", [c.NT, c.D], F32, kind="ExternalOutput").ap()
    b.xr = nc.dram_tensor("xr", [c.NT, c.D], F32).ap()
    xr8 = [Tok("xr%d" % i) for i in range(8)]
    b.XR = [xr8[i % 8] for i in range(c.NTT)]
    AR_F = 47872
    b.arena = b.stack.enter_context(nc.sbuf_tensor("arena", [128, AR_F], F32))
    b.small = b.stack.enter_context(nc.sbuf_tensor("small", [128, 4096], F32))
    b.psum = b.stack.enter_context(nc.psum_tensor("ps", [128, 8, 512], F32))
    b.PS = [Tok("ps%d" % i) for i in range(8)]
    b.soff = 0
    b.T_memin = Tok("memin")
    return b


def const_shapes(c):
    return {"ident": (128, 128), "amask": (17, 128, 128), "iota": (128, 4096)}


def salloc(b, n):
    o = b.soff
    b.soff += n
    assert b.soff <= 4096
    return b.small[:, o:o + n]


def dma(b, q, out, in_, rd, wr, tok, **kw):
    if len(out.shape) == 3 and out.shape[0] * out.shape[1] > 1024 and len(in_.shape) == 3:
        o = None
        prev = []
        for k0 in range(0, out.shape[1], 8):
            k1 = min(out.shape[1], k0 + 8)
            o = b.P.op(q, I("dma_start", out=out[:, k0:k1, :], in_=in_[:, k0:k1, :], **kw), rd=rd, wr=wr, dma=tok, skip=prev)
            prev.append(o)
        return o
    return b.P.op(q, I("dma_start", out=out, in_=in_, **kw), rd=rd, wr=wr, dma=tok)


def load_consts(b):
    c, P = b.c, b.P
    b.ident_f = salloc(b, 128)
    b.T_ident_f = Tok("identf")
    dma(b, "sp", b.ident_f, b.cst["ident"], [], [b.T_ident_f], b.T_ident_f)
    idb = salloc(b, 64)
    b.ident_b = idb.bitcast(BF16)
    b.T_ident_b = Tok("identb")
    P.op("dve", lambda e: e.tensor_copy(out=b.ident_b, in_=b.ident_f), rd=[b.T_ident_f], wr=[b.T_ident_b])
    b.ones_f = salloc(b, 128)
    b.T_ones = Tok("ones")
    P.op("pool", lambda e: e.memset(b.ones_f, 1.0), wr=[b.T_ones])
    onb = salloc(b, 64)
    b.ones_b = onb.bitcast(BF16)
    P.op("pool", lambda e: e.memset(b.ones_b, 1.0), wr=[b.T_ones])


def load_vec_pm(b, src_row, n, name):
    P = b.P
    k = n // 128
    dst = salloc(b, k)
    stg = salloc(b, 128)
    t_s, t_d = Tok(name + "_s"), Tok(name)
    dma(b, "sp", stg[0:k, :], src_row.rearrange("(k p) -> k p", p=128), [], [t_s], t_s)
    pst, PT = b.psum[:, 7, 0:k], b.PS[7]
    P.op("pe", lambda e: e.transpose(out=pst, in_=stg[0:k, :], identity=b.ident_f[0:k, 0:k]), rd=[t_s, b.T_ident_f], wr=[PT])
    P.op("dve", lambda e: e.tensor_copy(out=dst, in_=pst), rd=[PT], wr=[t_d])
    return dst, t_d


def norm_to_AT(b, src, src_toks, ntile, D, gain, gain_tok, AT, AT_toks, XT, HB, T_XT, T_HB, ssb, T_ss, eps=1e-6):
    P, c = b.P, b.c
    KC = D // 128
    for t in range(ntile):
        dma(b, "sp", XT[:, 0:D], src[t * 128:(t + 1) * 128, :], [src_toks[t]], [T_XT], T_XT)
        P.op("dve", lambda e: e.scalar_tensor_tensor(out=HB[:, 0:D], in0=XT[:, 0:D], scalar=1.0, in1=XT[:, 0:D],
                                                      op0=ALU.mult, op1=ALU.mult, accum_out=ssb[:, 0:1]),
             rd=[T_XT], wr=[T_HB, T_ss])
        P.op("act", lambda e: e.activation(out=ssb[:, 1:2], in_=ssb[:, 0:1], func=AF.Sqrt, scale=1.0 / D, bias=eps),
             rd=[T_ss], wr=[T_ss])
        P.op("dve", lambda e: e.reciprocal(out=ssb[:, 2:3], in_=ssb[:, 1:2]), rd=[T_ss], wr=[T_ss])
        P.op("act", lambda e: e.activation(out=HB[:, 0:D], in_=XT[:, 0:D], func=AF.Copy, scale=ssb[:, 2:3]),
             rd=[T_XT, T_ss], wr=[T_HB])
        for k0 in range(0, KC, 4):
            n = min(4, KC - k0)
            bank = 4 + (k0 // 4) % 2
            pst = b.psum[:, bank, 0:256].bitcast(BF16)[:, 0:n * 128]
            for j in range(n):
                P.op("pe", lambda e, j=j, k0=k0, pst=pst: e.transpose(out=pst[:, j * 128:(j + 1) * 128],
                                                                      in_=HB[:, (k0 + j) * 128:(k0 + j + 1) * 128],
                                                                      identity=b.ident_b),
                     rd=[T_HB, b.T_ident_b], wr=[b.PS[bank]])
            gb = gain[:, k0:k0 + n].unsqueeze(2).to_broadcast([128, n, 128])
            P.op("dve", lambda e, k0=k0, n=n, pst=pst, gb=gb, t=t: e.tensor_tensor(
                out=AT[:, k0:k0 + n, t * 128:(t + 1) * 128], in0=pst.rearrange("p (k t) -> p k t", k=n), in1=gb, op=ALU.mult),
                rd=[b.PS[bank], gain_tok], wr=[AT_toks[t]])


def gemm(b, A, A_toks, KCa, ntok, W, cols, mode, epi, Wt, T_Wt, wq="pool"):
    P = b.P
    Wv = W.rearrange("(k p) n -> p k n", p=128)
    for i, (c0, w) in enumerate(cols):
        s = i % len(Wt)
        wt, tw = Wt[s], T_Wt[s]
        dma(b, wq, wt[:, 0:KCa, 0:w], Wv[:, :, c0:c0 + w], [], [tw], tw)
        if mode == "FM":
            for cc in range(0, w, 128):
                for tb in range(0, ntok, 512):
                    nb = min(512, ntok - tb)
                    bank = b.gbank
                    b.gbank = (b.gbank + 1) % 4
                    acc = b.psum[:, bank, 0:nb]
                    for kc in range(KCa):
                        P.op("pe", lambda e, acc=acc, wt=wt, kc=kc, cc=cc, tb=tb, nb=nb: e.matmul(
                            acc, lhsT=wt[:, kc, cc:cc + 128], rhs=A[:, kc, tb:tb + nb], start=(kc == 0), stop=(kc == KCa - 1)),
                            rd=[tw] + A_toks[tb // 128:(tb + nb) // 128], wr=[b.PS[bank]])
                    epi(acc, b.PS[bank], c0 + cc, tb, nb)
        else:
            for tt in range(ntok // 128):
                bank = b.gbank
                b.gbank = (b.gbank + 1) % 4
                acc = b.psum[:, bank, 0:w]
                for kc in range(KCa):
                    P.op("pe", lambda e, acc=acc, wt=wt, kc=kc, tt=tt, w=w: e.matmul(
                        acc, lhsT=A[:, kc, tt * 128:(tt + 1) * 128], rhs=wt[:, kc, 0:w], start=(kc == 0), stop=(kc == KCa - 1)),
                        rd=[tw, A_toks[tt]], wr=[b.PS[bank]])
                epi(acc, b.PS[bank], c0, w, tt)


def colblocks(n0, n1, w=256):
    return [(c0, min(w, n1 - c0)) for c0 in range(n0, n1, w)]


def rmw_epi(b, tile0):
    P = b.P

    def epi(acc, tacc, c0, w, tt):
        i = b.rmw_i
        b.rmw_i = (i + 1) % len(b.RT)
        rt, trt = b.RT[i], b.T_RT[i]
        tx = b.XR[tile0 + tt]
        rows = b.xr[(tile0 + tt) * 128:(tile0 + tt + 1) * 128, c0:c0 + w]
        dma(b, "sp", rt[:, 0:w], rows, [tx], [trt], trt)
        P.op("dve", lambda e: e.tensor_tensor(out=rt[:, 0:w], in0=acc, in1=rt[:, 0:w], op=ALU.add), rd=[tacc, trt], wr=[trt])
        dma(b, "sp", rows, rt[:, 0:w], [trt], [tx], tx)
    return epi


def carve(b):
    c = b.c
    ar = b.arena
    KC, TG = c.KC, c.TG
    b.A0 = ar[:, 0:16384].bitcast(BF16)[:, 0:KC * TG].rearrange("p (k t) -> p k t", k=KC)
    b.A1 = ar[:, 16384:32768].bitcast(BF16)[:, 0:KC * TG].rearrange("p (k t) -> p k t", k=KC)
    b.Wt = [ar[:, 32768 + i * 4096:32768 + (i + 1) * 4096].bitcast(BF16).rearrange("p (k n) -> p k n", n=256) for i in range(2)]
    b.XT = ar[:, 40960:45056]
    b.HB = ar[:, 45056:47104].bitcast(BF16)
    b.RT = [ar[:, 47104 + i * 256:47104 + (i + 1) * 256] for i in range(3)]
    ntg = TG // 128
    b.T_A0 = [Tok("a0_%d" % i) for i in range(ntg)]
    b.T_A1 = [Tok("a1_%d" % i) for i in range(ntg)]
    b.T_Wt = [Tok("wt%d" % i) for i in range(2)]
    b.T_XT, b.T_HB = Tok("xt"), Tok("hb")
    b.T_RT = [Tok("rt%d" % i) for i in range(3)]
    b.ssb = salloc(b, 4)
    b.T_ss = Tok("ss")
    b.gbank = 0
    b.rmw_i = 0
    b.FT = [salloc(b, 512) for i in range(3)]
    b.T_FT = [Tok("ft%d" % i) for i in range(3)]
    b.PTt = [salloc(b, 256).bitcast(BF16) for i in range(2)]
    b.T_PTt = [Tok("pt%d" % i) for i in range(2)]
    b.ft_i = 0


def mlp_phase(b, l, g, gain, gain_tok):
    c, P = b.c, b.P
    ntg = c.TG // 128
    t0 = g * ntg
    norm_to_AT(b, b.xr[g * c.TG:(g + 1) * c.TG, :], b.XR[t0:t0 + ntg], ntg, c.D, gain, gain_tok,
               b.A0, b.T_A0, b.XT, b.HB, b.T_XT, b.T_HB, b.ssb, b.T_ss)
    D = c.D

    def relu2_epi(acc, tacc, col, tb, nb):
        i = b.ft_i
        b.ft_i = (i + 1) % len(b.FT)
        ft, tft = b.FT[i], b.T_FT[i]
        j = (col % D) // 128
        P.op("act", lambda e: e.activation(out=ft[:, 0:nb], in_=acc, func=AF.Relu), rd=[tacc], wr=[tft])
        P.op("pool", lambda e: e.tensor_tensor(out=b.A1[:, j, tb:tb + nb], in0=ft[:, 0:nb], in1=ft[:, 0:nb], op=ALU.mult),
             rd=[tft], wr=b.T_A1[tb // 128:(tb + nb) // 128])

    for q in range(4):
        gemm(b, b.A0, b.T_A0, c.KC, c.TG, b.prm["w_up"][l], colblocks(q * D, (q + 1) * D), "FM", relu2_epi, b.Wt, b.T_Wt)
        gemm(b, b.A1, b.T_A1, c.KC, c.TG, b.prm["w_down"][l][q * D:(q + 1) * D, :], colblocks(0, D), "TM",
             rmw_epi(b, t0), b.Wt, b.T_Wt)


def copy_rows(b, dst, dst_toks, src, src_toks, ntile):
    for t in range(ntile):
        dma(b, "sp", dst[t * 128:(t + 1) * 128, :], src[t * 128:(t + 1) * 128, :],
            [src_toks[t]] if src_toks else [], [dst_toks[t]], dst_toks[t])


def build(c, phases=("mix", "attn", "conv", "ssm", "cross", "mlp"), dbg=()):
    b = _setup(c, dbg)
    P = b.P
    load_consts(b)
    carve(b)
    b.T_OUT = [Tok("out")] * c.NTT
    copy_rows(b, b.xr, b.XR, b.x_in, None, c.NTT)
    for l in range(c.depth):
        soff0 = b.soff
        if "mix" in phases:
            mix_setup(b)
            gx, tgx = load_vec(b, "g_mix", b.prm["norm_mix"][l], c.D)
            go, tgo = load_vec(b, "g_mo", b.prm["mix_out_norm"][l], c.D)
            qn, tqn = load_vec(b, "g_q", b.prm["q_norm"][l], 128)
            kn, tkn = load_vec(b, "g_k", b.prm["k_norm"][l], 128)
            qns, tqns = palloc(b, "g_qs", 1)
            P.op("dve", I("tensor_scalar", out=qns, in0=qn, scalar1=128.0 ** -0.5, scalar2=None, op0=ALU.mult), rd=[tqn], wr=[tqns])
            for g in range(c.NG):
                inproj_phase(b, l, g, gx, tgx, qns, tqns, kn, tkn)
            if "attn" in phases:
                attn_phase(b, l)
                attn_norm_phase(b, l, go, tgo)
            if "conv" in phases:
                conv_phase(b, l, go, tgo)
            if "ssm" in phases:
                ssm_phase(b, l, go, tgo)
            if not all(p in phases for p in ("attn", "conv", "ssm")):
                zt = b.HB[:, 0:c.NT]
                P.op("pool", I("memset", zt, 0.0), wr=[b.T_HB])
                for nm, r0, r1 in (("attn", 0, c.AW), ("conv", c.AW, c.AW + c.CW), ("ssm", c.AW + c.CW, c.D)):
                    if nm not in phases:
                        for r in range(r0, r1, 128):
                            dma(b, "sp", b.mixT[r:r + 128, :], zt, [b.T_HB], [b.T_mixT], b.T_mixT)
            for g in range(c.NG):
                wout_phase(b, l, g)
        if "cross" in phases:
            cross_phase_prologue(b, l)
            gc, tgc = load_vec(b, "g_cross", b.prm["norm_cross"][l], c.D)
            cq, tcq = load_vec(b, "g_cq", b.prm["cq_norm"][l], 128)
            cqs, tcqs = palloc(b, "g_cqs", 1)
            P.op("dve", lambda e: e.tensor_scalar(out=cqs, in0=cq, scalar1=128.0 ** -0.5, scalar2=None, op0=ALU.mult), rd=[tcq], wr=[tcqs])
            for g in range(c.NG):
                cross_phase(b, l, g, gc, tgc, cqs, tcqs)
        if "mlp" in phases:
            gm, tgm = load_vec(b, "g_mlp", b.prm["norm_mlp"][l], c.D)
            for g in range(c.NG):
                mlp_phase(b, l, g, gm, tgm)
    copy_rows(b, b.out, b.T_OUT, b.xr, b.XR, c.NTT)
    P.op("sp", None, rd=b.T_OUT)
    P.emit(b.nc, b.stack)
    b.stack.close()
    return b.nc


def make_consts(c):
    ident = np.eye(128, dtype=np.float32)
    i = np.arange(128)
    am = np.zeros((17, 128, 128), np.float32)
    for dl in range(17):
        diff = 128 * dl + i[None, :] - i[:, None]
        for d in (1, 4, 16):
            am[dl] += ((diff >= 0) & (diff % d == 0) & (diff // d <= 128)).astype(np.float32)
    iota = np.broadcast_to(np.arange(4096, dtype=np.float32), (128, 4096)).copy()
    return {"ident": ident, "amask": am, "iota": iota}


_CACHE = {}


def kernel(**inputs):
    c = Cfg()
    x = np.asarray(inputs["x"], dtype=np.float32)
    nb = x.shape[0]
    if "nc" not in _CACHE:
        _CACHE["nc"] = build(c)
    nc = _CACHE["nc"]
    consts = make_consts(c)
    shp = param_shapes(c)
    shared = {n: np.ascontiguousarray(np.asarray(inputs[n], dtype=np.float32).reshape([c.depth] + list(shp[n]))) for n in shp}
    shared.update(consts)
    in_maps = []
    for i in range(nb):
        m = dict(shared)
        m["x"] = np.ascontiguousarray(x[i])
        m["mem"] = np.ascontiguousarray(np.asarray(inputs["mem"], dtype=np.float32)[i])
        in_maps.append(m)
    res = run_bass_kernel_spmd(nc, in_maps, core_ids=list(range(nb)))
    return np.stack([np.asarray(r["out"], dtype=np.float32) for r in res.results], axis=0)


def palloc(b, name, n):
    d = b.__dict__.setdefault("_pa", {})
    if name not in d:
        d[name] = (salloc(b, n), Tok(name))
    return d[name]


def load_mat_pm(b, name, src, r, off=0):
    P = b.P
    dst, t_d = palloc(b, name, r)
    stg, t_s = palloc(b, "stg", 128)
    dma(b, "sp", stg[0:r, :], src, [], [t_s], t_s)
    pst, PT = b.psum[:, 7, 0:r], b.PS[7]
    P.op("pe", lambda e: e.transpose(out=pst, in_=stg[0:r, :], identity=b.ident_f[0:r, 0:r]), rd=[t_s, b.T_ident_f], wr=[PT])
    P.op("dve", lambda e: e.tensor_copy(out=dst, in_=pst), rd=[PT], wr=[t_d])
    return dst, t_d


def load_vec(b, name, src_row, n):
    return load_mat_pm(b, name, src_row.rearrange("(k p) -> k p", p=128), n // 128)


def ft(b):
    i = b.ft_i
    b.ft_i = (i + 1) % len(b.FT)
    return b.FT[i], b.T_FT[i]


def fm_rms_scale(b, acc, tacc, nb, gain_col, gain_tok, scale, out, out_toks, n_feat=128):
    P = b.P
    sq, tsq = ft(b)
    P.op("act", lambda e: e.activation(out=sq[:, 0:nb], in_=acc, func=AF.Square), rd=[tacc], wr=[tsq])
    ssp, tss = b.psum[:, 6, 0:nb], b.PS[6]
    P.op("pe", lambda e: e.matmul(ssp, lhsT=b.ones_f, rhs=sq[:, 0:nb], start=True, stop=True), rd=[tsq, b.T_ones], wr=[tss])
    P.op("act", lambda e: e.activation(out=sq[:, 0:nb], in_=ssp, func=AF.Sqrt, scale=1.0 / n_feat, bias=1e-6), rd=[tss], wr=[tsq])
    P.op("dve", lambda e: e.reciprocal(out=sq[:, 0:nb], in_=sq[:, 0:nb]), rd=[tsq], wr=[tsq])
    P.op("dve", lambda e: e.tensor_tensor(out=sq[:, 0:nb], in0=acc, in1=sq[:, 0:nb], op=ALU.mult), rd=[tacc, tsq], wr=[tsq])
    P.op("act", lambda e: e.activation(out=out, in_=sq[:, 0:nb], func=AF.Copy, scale=gain_col), rd=[tsq, gain_tok], wr=out_toks)
    if scale != 1.0:
        raise NotImplementedError


def cross_phase_prologue(b, l):
    c, P = b.c, b.P
    gm, tgm = load_vec(b, "g_mem", b.prm["norm_mem"][l], c.D)
    ck, tck = load_vec(b, "g_ck", b.prm["ck_norm"][l], 128)
    nmt = c.ML // 128
    norm_to_AT(b, b.mem_in, [b.T_memin] * nmt, nmt, c.D, gm, tgm, b.A1, b.T_A1, b.XT, b.HB, b.T_XT, b.T_HB, b.ssb, b.T_ss)
    mk, tmk = palloc(b, "memK", 4 * c.ML // 2)
    mv, tmv = palloc(b, "memV", nmt * 512 // 2)
    b.memK = mk.bitcast(BF16).rearrange("p (h m) -> p h m", h=4)
    b.memV = mv.bitcast(BF16).rearrange("p (t d) -> p t d", t=nmt)
    b.T_memK, b.T_memV = tmk, tmv

    def k_epi(acc, tacc, col, tb, nb):
        h = col // 128
        fm_rms_scale(b, acc, tacc, nb, ck[:, 0:1], tck, 1.0, b.memK[:, h, tb:tb + nb], [tmk])

    def v_epi(acc, tacc, c0, w, tt):
        P.op("act", lambda e: e.activation(out=b.memV[:, tt, c0 - 512:c0 - 512 + w], in_=acc, func=AF.Copy), rd=[tacc], wr=[tmv])

    gemm(b, b.A1, b.T_A1, c.KC, c.ML, b.prm["w_ckv"][l], colblocks(0, 512), "FM", k_epi, b.Wt, b.T_Wt)
    gemm(b, b.A1, b.T_A1, c.KC, c.ML, b.prm["w_ckv"][l], colblocks(512, 1024), "TM", v_epi, b.Wt, b.T_Wt)


def cross_phase(b, l, g, gain, gain_tok, cq, tcq):
    c, P = b.c, b.P
    ntg = c.TG // 128
    t0 = g * ntg
    TG = c.TG
    norm_to_AT(b, b.xr[g * TG:(g + 1) * TG, :], b.XR[t0:t0 + ntg], ntg, c.D, gain, gain_tok,
               b.A0, b.T_A0, b.XT, b.HB, b.T_XT, b.T_HB, b.ssb, b.T_ss)
    qc = b.HB[:, 0:4 * TG].rearrange("p (h t) -> p h t", h=4)

    def q_epi(acc, tacc, col, tb, nb):
        h = col // 128
        fm_rms_scale(b, acc, tacc, nb, cq[:, 0:1], tcq, 1.0, qc[:, h, tb:tb + nb], [b.T_HB])

    gemm(b, b.A0, b.T_A0, c.KC, TG, b.prm["w_cq"][l], colblocks(0, 512), "FM", q_epi, b.Wt, b.T_Wt)
    nmt = c.ML // 128
    for h in range(4):
        for tb in range(0, TG, 512):
            pts = []
            for mt in range(nmt):
                bank = b.gbank
                b.gbank = (b.gbank + 1) % 4
                sc = b.psum[:, bank, 0:512]
                P.op("pe", I("matmul", sc, lhsT=b.memK[:, h, mt * 128:(mt + 1) * 128], rhs=qc[:, h, tb:tb + 512],
                             start=True, stop=True), rd=[b.T_memK, b.T_HB], wr=[b.PS[bank]])
                pt, tpt = b.PTt[mt % 2], b.T_PTt[mt % 2]
                P.op("act", I("activation", out=pt, in_=sc, func=AF.Exp), rd=[b.PS[bank]], wr=[tpt])
                pts.append((pt, tpt))
            den, tden = b.psum[:, 6, 0:512], b.PS[6]
            num, tnum = b.psum[:, 7, 0:512], b.PS[7]
            for mt in range(nmt):
                pt, tpt = pts[mt]
                P.op("pe", I("matmul", den, lhsT=b.ones_b, rhs=pt, start=(mt == 0), stop=(mt == nmt - 1)),
                     rd=[tpt, b.T_ones], wr=[tden])
            for mt in range(nmt):
                pt, tpt = pts[mt]
                P.op("pe", I("matmul", num, lhsT=b.memV[:, mt, h * 128:(h + 1) * 128], rhs=pt,
                             start=(mt == 0), stop=(mt == nmt - 1)), rd=[tpt, b.T_memV], wr=[tnum])
            rd_, trd = ft(b)
            P.op("dve", I("reciprocal", out=rd_, in_=den), rd=[tden], wr=[trd])
            P.op("dve", I("tensor_tensor", out=b.A1[:, h, tb:tb + 512], in0=num, in1=rd_, op=ALU.mult),
                 rd=[tnum, trd], wr=b.T_A1[tb // 128:tb // 128 + 4])
    gemm(b, b.A1, b.T_A1, 4, TG, b.prm["w_co"][l], colblocks(0, c.D), "TM", rmw_epi(b, t0), b.Wt, b.T_Wt)


def mix_setup(b):
    c, nc = b.c, b.nc
    if hasattr(b, "qT"):
        return
    b.qT = nc.dram_tensor("qT", [c.H, 128, c.NT], BF16).ap()
    b.kT = nc.dram_tensor("kT", [c.H, 128, c.NT], BF16).ap()
    b.vS = nc.dram_tensor("vS", [c.NT, c.AW], BF16).ap()
    b.hcS = nc.dram_tensor("hcS", [c.CW, c.NT], F32).ap()
    b.uS = nc.dram_tensor("uS", [c.SW, c.NT], F32).ap()
    b.attS = nc.dram_tensor("attS", [c.NT, c.AW], F32).ap()
    b.zS = nc.dram_tensor("zS", [c.SW, c.NT], F32).ap()
    b.mixT = nc.dram_tensor("mixT", [c.D, c.NT], BF16).ap()
    for n in ("qT", "kT", "vS", "hcS", "uS", "attS", "zS", "mixT"):
        setattr(b, "T_" + n, Tok(n))


class MR:
    def __init__(self, b, lo=16384):
        self.b = b
        self.off = lo
        self.lo = lo
        self.toks = []
        self.al = (list(b.T_A0) if lo == 0 else []) + list(b.T_A1)

    def f32(self, n, name):
        o = self.off
        self.off += n
        assert self.off <= 32768, name
        t = Tok(name, alias=self.al)
        self.toks.append(t)
        return self.b.arena[:, o:o + n], t

    def bf16(self, n, name):
        a, t = self.f32((n + 1) // 2, name)
        return a.bitcast(BF16)[:, 0:n], t

    def close(self):
        self.b.T_A1 = [Tok("a1_%d" % i, alias=self.toks + self.b.T_A1) for i in range(len(self.b.T_A1))]
        if self.lo == 0:
            self.b.T_A0 = [Tok("a0_%d" % i, alias=self.toks + self.b.T_A0) for i in range(len(self.b.T_A0))]


def inproj_phase(b, l, g, gain, tgain, qg, tqg, kg, tkg):
    c, P = b.c, b.P
    TG, AW, CW = c.TG, c.AW, c.CW
    ntg = TG // 128
    t0 = g * ntg
    norm_to_AT(b, b.xr[g * TG:(g + 1) * TG, :], b.XR[t0:t0 + ntg], ntg, c.D, gain, tgain,
               b.A0, b.T_A0, b.XT, b.HB, b.T_XT, b.T_HB, b.ssb, b.T_ss)
    W = b.prm["w_in"][l]
    mr = MR(b)
    SG = [mr.f32(512, "sg%d" % i) for i in range(TG // 512)]
    st = {"i": 0}

    def obuf():
        i = st["i"]
        st["i"] = (i + 1) % 2
        return b.PTt[i], b.T_PTt[i]

    def qk_epi(scr, T_scr, gcol, tg_):
        def epi(acc, tacc, col, tb, nb):
            h = (col % AW) // 128
            ob, tob = obuf()
            fm_rms_scale(b, acc, tacc, nb, gcol, tg_, 1.0, ob[:, 0:nb], [tob])
            dma(b, "sp", scr[h, :, g * TG + tb:g * TG + tb + nb], ob[:, 0:nb], [tob], [T_scr], T_scr)
        return epi

    gemm(b, b.A0, b.T_A0, c.KC, TG, W, colblocks(0, AW), "FM", qk_epi(b.qT, b.T_qT, qg, tqg), b.Wt, b.T_Wt)
    gemm(b, b.A0, b.T_A0, c.KC, TG, W, colblocks(AW, 2 * AW), "FM", qk_epi(b.kT, b.T_kT, kg, tkg), b.Wt, b.T_Wt)

    def v_epi(acc, tacc, c0, w, tt):
        ob, tob = obuf()
        P.op("act", I("activation", out=ob[:, 0:w], in_=acc, func=AF.Copy), rd=[tacc], wr=[tob])
        dma(b, "sp", b.vS[(t0 + tt) * 128:(t0 + tt + 1) * 128, c0 - 2 * AW:c0 - 2 * AW + w], ob[:, 0:w], [tob], [b.T_vS], b.T_vS)

    gemm(b, b.A0, b.T_A0, c.KC, TG, W, colblocks(2 * AW, 3 * AW), "TM", v_epi, b.Wt, b.T_Wt)
    cols = []
    for cc in range(c.CCH):
        cols += [(3 * AW + CW + cc * 128, 128), (3 * AW + cc * 128, 128)]

    def conv_epi(acc, tacc, col, tb, nb):
        rel = col - 3 * AW
        sg, tsg = SG[tb // 512]
        if rel >= CW:
            P.op("act", I("activation", out=sg[:, 0:nb], in_=acc, func=AF.Sigmoid), rd=[tacc], wr=[tsg])
        else:
            f, tf = ft(b)
            P.op("dve", I("tensor_tensor", out=f[:, 0:nb], in0=acc, in1=sg[:, 0:nb], op=ALU.mult), rd=[tacc, tsg], wr=[tf])
            dma(b, "sp", b.hcS[rel:rel + 128, g * TG + tb:g * TG + tb + nb], f[:, 0:nb], [tf], [b.T_hcS], b.T_hcS)

    gemm(b, b.A0, b.T_A0, c.KC, TG, W, cols, "FM", conv_epi, b.Wt, b.T_Wt)

    def u_epi(acc, tacc, col, tb, nb):
        f, tf = ft(b)
        rel = col - 3 * AW - 2 * CW
        P.op("act", I("activation", out=f[:, 0:nb], in_=acc, func=AF.Copy), rd=[tacc], wr=[tf])
        dma(b, "sp", b.uS[rel:rel + 128, g * TG + tb:g * TG + tb + nb], f[:, 0:nb], [tf], [b.T_uS], b.T_uS)

    gemm(b, b.A0, b.T_A0, c.KC, TG, W, colblocks(3 * AW + 2 * CW, c.INW), "FM", u_epi, b.Wt, b.T_Wt)
    mr.close()


def attn_phase(b, l):
    c, P = b.c, b.P
    NT, NTT, AW = c.NT, c.NTT, c.AW
    mr = MR(b)
    qh, tq = mr.bf16(NT, "qh")
    kh, tk = mr.bf16(NT, "kh")
    vx, tv = mr.bf16(NTT * 130, "vx")
    vx = vx.rearrange("p (t d) -> p t d", d=130)
    mk, tm = mr.bf16(17 * 128, "mask")
    mk3 = mk.rearrange("p (d q) -> p d q", d=17)
    mstg, tms = mr.f32(17 * 128, "mstg")
    dma(b, "sp", mstg.rearrange("p (d q) -> p d q", d=17), b.cst["amask"].rearrange("d k q -> k d q"), [], [tms], tms)
    P.op("dve", I("tensor_copy", out=mk, in_=mstg), rd=[tms], wr=[tm])
    P.op("pool", I("memset", vx[:, :, 128:130], 1.0), wr=[tv])
    ET = [mr.bf16(512, "et%d" % i) for i in range(2)]
    PT = [mr.bf16(512, "ptm%d" % i) for i in range(2)]
    OT = [mr.f32(128, "ot%d" % i) for i in range(2)]
    RD = [mr.f32(1, "rd%d" % i) for i in range(2)]
    ei = 0
    for h in range(c.H):
        dma(b, "sp", qh, b.qT[h], [b.T_qT], [tq], tq)
        dma(b, "sp", kh, b.kT[h], [b.T_kT], [tk], tk)
        dma(b, "sp", vx[:, :, 0:128], b.vS[:, h * 128:(h + 1) * 128].rearrange("(t p) d -> p t d", p=128), [b.T_vS], [tv], tv)
        for qt in range(NTT):
            nd = min(16, qt) + 1
            ob = 4 + (qt % 2)
            oacc, toacc = b.psum[:, ob, 0:129], b.PS[ob]
            for d0 in range(0, nd, 4):
                n = min(4, nd - d0)
                bank = b.gbank
                b.gbank = (b.gbank + 1) % 4
                sc, tsc = b.psum[:, bank, 0:n * 128], b.PS[bank]
                for j in range(n):
                    kt = qt - (d0 + j)
                    P.op("pe", I("matmul", sc[:, j * 128:(j + 1) * 128], lhsT=kh[:, kt * 128:(kt + 1) * 128],
                                 rhs=qh[:, qt * 128:(qt + 1) * 128], start=True, stop=True), rd=[tk, tq], wr=[tsc])
                et, tet = ET[ei % 2]
                pt, tpt = PT[ei % 2]
                ei += 1
                P.op("act", I("activation", out=et[:, 0:n * 128], in_=sc, func=AF.Exp), rd=[tsc], wr=[tet])
                P.op("pool", I("tensor_tensor", out=pt[:, 0:n * 128], in0=et[:, 0:n * 128], in1=mk[:, d0 * 128:(d0 + n) * 128], op=ALU.mult),
                     rd=[tet, tm], wr=[tpt])
                for j in range(n):
                    kt = qt - (d0 + j)
                    P.op("pe", I("matmul", oacc, lhsT=pt[:, j * 128:(j + 1) * 128], rhs=vx[:, kt, 0:129],
                                 start=(d0 + j == 0), stop=(d0 + j == nd - 1)), rd=[tpt, tv], wr=[toacc])
            rd_, trd = RD[qt % 2]
            ot, tot = OT[qt % 2]
            P.op("dve", I("reciprocal", out=rd_, in_=oacc[:, 128:129]), rd=[toacc], wr=[trd])
            P.op("act", I("activation", out=ot, in_=oacc[:, 0:128], func=AF.Copy, scale=rd_), rd=[toacc, trd], wr=[tot])
            dma(b, "sp", b.attS[qt * 128:(qt + 1) * 128, h * 128:(h + 1) * 128], ot, [tot], [b.T_attS], b.T_attS)
    mr.close()


def attn_norm_phase(b, l, gain, tgain):
    c, P = b.c, b.P
    mr = MR(b)
    nk = c.AW // 128
    ST = [mr.bf16(nk * 128, "ast%d" % i) for i in range(2)]
    for t in range(c.NTT):
        stg, tst = ST[t % 2]
        st3 = stg.rearrange("p (k t) -> p k t", k=nk)
        norm_to_AT(b, b.attS[t * 128:(t + 1) * 128, :], [b.T_attS], 1, c.AW, gain, tgain, st3, [tst],
                   b.XT, b.HB, b.T_XT, b.T_HB, b.ssb, b.T_ss)
        dma(b, "sp", b.mixT[0:c.AW, t * 128:(t + 1) * 128].rearrange("(k p) t -> p k t", p=128), st3, [tst], [b.T_mixT], b.T_mixT)
    mr.close()


def wout_phase(b, l, g):
    c = b.c
    TG = c.TG
    ntg = TG // 128
    for t in range(ntg):
        dma(b, "sp", b.A0[:, :, t * 128:(t + 1) * 128],
            b.mixT[:, g * TG + t * 128:g * TG + (t + 1) * 128].rearrange("(k p) t -> p k t", p=128), [b.T_mixT], [b.T_A0[t]], b.T_A0[t])
    gemm(b, b.A0, b.T_A0, c.KC, TG, b.prm["w_out"][l], colblocks(0, c.D), "TM", rmw_epi(b, g * ntg), b.Wt, b.T_Wt)


def group_rms_store(b, Ys, S3, tS3, R, tR, width, gain, gcol0, tgain, row0, t_0, nb):
    P = b.P
    P.op("act", I("activation", out=R[:, 0:nb], in_=S3, func=AF.Sqrt, scale=1.0 / width, bias=1e-6), rd=[tS3], wr=[tR])
    P.op("dve", I("reciprocal", out=R[:, 0:nb], in_=R[:, 0:nb]), rd=[tR], wr=[tR])
    for cc, (y, ty) in enumerate(Ys):
        f, tf = ft(b)
        i = cc % 2
        ob, tob = b.PTt[i], b.T_PTt[i]
        P.op("dve", I("tensor_tensor", out=f[:, 0:nb], in0=y, in1=R[:, 0:nb], op=ALU.mult), rd=[ty, tR], wr=[tf])
        P.op("act", I("activation", out=ob[:, 0:nb], in_=f[:, 0:nb], func=AF.Copy, scale=gain[:, gcol0 + cc:gcol0 + cc + 1]),
             rd=[tf, tgain], wr=[tob])
        dma(b, "sp", b.mixT[row0 + cc * 128:row0 + (cc + 1) * 128, t_0:t_0 + nb], ob[:, 0:nb], [tob], [b.T_mixT], b.T_mixT)


def sumsq_acc(b, y, ty, S, tS, first, last, nb):
    P = b.P
    sq, tsq = ft(b)
    P.op("pool", I("tensor_tensor", out=sq[:, 0:nb], in0=y, in1=y, op=ALU.mult), rd=[ty], wr=[tsq])
    P.op("pe", I("matmul", S, lhsT=b.ones_f, rhs=sq[:, 0:nb], start=first, stop=last), rd=[tsq, b.T_ones], wr=[tS])


def conv_phase(b, l, go, tgo):
    c, P = b.c, b.P
    NT, CW, CCH, AW = c.NT, c.CW, c.CCH, c.AW
    mr = MR(b)
    HX = [mr.f32(544, "hx%d" % i) for i in range(2)]
    Y = [mr.f32(512, "cy%d" % i) for i in range(CCH)]
    M, tM = mr.f32(512, "cm")
    R, tR = mr.f32(512, "cr")
    dw = [load_mat_pm(b, "dw%d" % cc, b.prm["conv_dw"][l][:, cc * 128:(cc + 1) * 128], 31) for cc in range(CCH)]
    cb, tcb = load_vec(b, "c_b", b.prm["conv_b"][l], CW)
    lg, tlg = load_vec(b, "c_lg", b.prm["conv_ln_g"][l], CW)
    lb, tlb = load_vec(b, "c_lb", b.prm["conv_ln_b"][l], CW)
    for tbk in range(NT // 512):
        t_0 = tbk * 512
        S1, tS1 = b.psum[:, 6, 0:512], b.PS[6]
        S2, tS2 = b.psum[:, 7, 0:512], b.PS[7]
        for cc in range(CCH):
            hx, thx = HX[cc % 2]
            rows = b.hcS[cc * 128:(cc + 1) * 128, :]
            if tbk == 0:
                P.op("pool", I("memset", hx[:, 0:30], 0.0), wr=[thx])
                dma(b, "sp", hx[:, 30:542], rows[:, 0:512], [b.T_hcS], [thx], thx)
            else:
                dma(b, "sp", hx[:, 0:542], rows[:, t_0 - 30:t_0 + 512], [b.T_hcS], [thx], thx)
            y, ty = Y[cc]
            d, td = dw[cc]
            P.op("dve", I("tensor_scalar", out=y, in0=hx[:, 0:512], scalar1=d[:, 0:1], scalar2=cb[:, cc:cc + 1],
                          op0=ALU.mult, op1=ALU.add), rd=[thx, td, tcb], wr=[ty])
            for j in range(1, 31):
                P.op("dve", I("scalar_tensor_tensor", out=y, in0=hx[:, j:j + 512], scalar=d[:, j:j + 1], in1=y,
                              op0=ALU.mult, op1=ALU.add), rd=[thx, td, ty], wr=[ty])
            P.op("pe", I("matmul", S1, lhsT=b.ones_f, rhs=y, start=(cc == 0), stop=(cc == CCH - 1)), rd=[ty, b.T_ones], wr=[tS1])
            sumsq_acc(b, y, ty, S2, tS2, cc == 0, cc == CCH - 1, 512)
        msq, tmsq = ft(b)
        var, tvar = ft(b)
        P.op("act", I("activation", out=M, in_=S1, func=AF.Copy, scale=1.0 / CW), rd=[tS1], wr=[tM])
        P.op("pool", I("tensor_tensor", out=msq, in0=M, in1=M, op=ALU.mult), rd=[tM], wr=[tmsq])
        P.op("dve", I("scalar_tensor_tensor", out=var, in0=S2, scalar=1.0 / CW, in1=msq, op0=ALU.mult, op1=ALU.subtract),
             rd=[tS2, tmsq], wr=[tvar])
        P.op("act", I("activation", out=R, in_=var, func=AF.Sqrt, bias=1e-6), rd=[tvar], wr=[tR])
        P.op("dve", I("reciprocal", out=R, in_=R), rd=[tR], wr=[tR])
        S3, tS3 = b.psum[:, 6, 0:512], b.PS[6]
        for cc in range(CCH):
            y, ty = Y[cc]
            sg, tsg = ft(b)
            P.op("dve", I("tensor_tensor", out=y, in0=y, in1=M, op=ALU.subtract), rd=[ty, tM], wr=[ty])
            P.op("dve", I("tensor_tensor", out=y, in0=y, in1=R, op=ALU.mult), rd=[ty, tR], wr=[ty])
            P.op("act", I("activation", out=sg, in_=y, func=AF.Sigmoid, scale=lg[:, cc:cc + 1], bias=lb[:, cc:cc + 1]),
                 rd=[ty, tlg, tlb], wr=[tsg])
            P.op("dve", I("tensor_scalar", out=y, in0=y, scalar1=lg[:, cc:cc + 1], scalar2=lb[:, cc:cc + 1], op0=ALU.mult, op1=ALU.add),
                 rd=[ty, tlg, tlb, tsg], wr=[ty])
            P.op("pool", I("tensor_tensor", out=y, in0=y, in1=sg, op=ALU.mult), rd=[ty, tsg], wr=[ty])
            sumsq_acc(b, y, ty, S3, tS3, cc == 0, cc == CCH - 1, 512)
        group_rms_store(b, Y, S3, tS3, R, tR, CW, go, AW // 128, tgo, AW, t_0, 512)
    mr.close()


def nbank(b):
    k = b.gbank
    b.gbank = (k + 1) % 4
    return b.psum[:, k, 0:512], b.PS[k]


def ssm_phase(b, l, go, tgo):
    c, P = b.c, b.P
    NT, SW, SCH, AW, CW = c.NT, c.SW, c.SCH, c.AW, c.CW
    G = SW // 16
    LOG = int(math.log2(NT))
    CH = min(1024, NT)
    mr = MR(b, 0)
    prm = b.prm

    def load_dup(name, src):
        stg, ts = mr.f32(128, name + "s")
        dst, td = mr.f32(G, name)
        dma(b, "sp", stg[0:G, 0:64], src, [], [ts], ts)
        dma(b, "sp", stg[0:G, 64:128], src, [], [ts], ts)
        P.op("pe", I("transpose", out=b.psum[:, 7, 0:G], in_=stg[0:G, :], identity=b.ident_f[0:G, 0:G]), rd=[ts, b.T_ident_f], wr=[b.PS[7]])
        P.op("dve", I("tensor_copy", out=dst, in_=b.psum[:, 7, 0:G]), rd=[b.PS[7]], wr=[td])
        return dst, td

    are, tare = load_dup("are", prm["ssm_a_re"][l])
    aim, taim = load_dup("aim", prm["ssm_a_im"][l])
    stp, tstp = mr.f32(G, "stp")
    dma(b, "sp", stp.rearrange("p (o g) -> p o g", o=1), prm["ssm_log_step"][l:l + 1, :].partition_broadcast(128), [], [tstp], tstp)
    P.op("act", I("activation", out=stp, in_=stp, func=AF.Exp), rd=[tstp], wr=[tstp])
    TH, tTH = mr.f32(G, "th")
    RHO, tRHO = mr.f32(G, "rho")
    P.op("dve", I("tensor_tensor", out=TH, in0=aim, in1=stp, op=ALU.mult), rd=[taim, tstp], wr=[tTH])
    P.op("dve", I("tensor_tensor", out=RHO, in0=are, in1=stp, op=ALU.mult), rd=[tare, tstp], wr=[tRHO])
    P.op("act", I("activation", out=RHO, in_=RHO, func=AF.Exp), rd=[tRHO], wr=[tRHO])
    NL = 4 + LOG
    CA, tCA = mr.f32(NL * G, "ca")
    SA, tSA = mr.f32(NL * G, "sa")
    CA3 = CA.rearrange("p (k g) -> p k g", g=G)
    SA3 = SA.rearrange("p (k g) -> p k g", g=G)
    hp, thp = palloc(b, "halfpi", 1)
    P.op("pool", I("memset", hp, math.pi / 2), wr=[thp])
    P.op("act", I("activation", out=SA3[:, 0, :], in_=TH, func=AF.Sin, scale=1.0 / 16), rd=[tTH], wr=[tSA])
    P.op("act", I("activation", out=CA3[:, 0, :], in_=TH, func=AF.Sin, scale=1.0 / 16, bias=hp[:, 0:1]), rd=[tTH, thp], wr=[tCA])
    t1, tt1 = mr.f32(G, "t1")
    t2, tt2 = mr.f32(G, "t2")
    for i in range(NL - 1):
        P.op("dve", I("tensor_tensor", out=t1, in0=CA3[:, i, :], in1=CA3[:, i, :], op=ALU.mult), rd=[tCA], wr=[tt1])
        P.op("pool", I("tensor_tensor", out=t2, in0=SA3[:, i, :], in1=SA3[:, i, :], op=ALU.mult), rd=[tSA], wr=[tt2])
        P.op("dve", I("tensor_tensor", out=CA3[:, i + 1, :], in0=t1, in1=t2, op=ALU.subtract), rd=[tt1, tt2], wr=[tCA])
        P.op("dve", I("scalar_tensor_tensor", out=SA3[:, i + 1, :], in0=SA3[:, i, :], scalar=2.0, in1=CA3[:, i, :],
                      op0=ALU.mult, op1=ALU.mult), rd=[tSA, tCA], wr=[tSA])
    KR, tKR = mr.f32(G, "kr")
    KI, tKI = mr.f32(G, "ki")
    KIN, tKIN = mr.f32(G, "kin")
    KRN, tKRN = mr.f32(G, "krn")
    nr, tnr = mr.f32(G, "nr")
    ni, tni = mr.f32(G, "ni")
    dn, tdn = mr.f32(G, "dn")
    P.op("dve", I("tensor_tensor", out=nr, in0=RHO, in1=CA3[:, 4, :], op=ALU.mult), rd=[tRHO, tCA], wr=[tnr])
    P.op("dve", I("tensor_scalar", out=nr, in0=nr, scalar1=-1.0, scalar2=None, op0=ALU.add), rd=[tnr], wr=[tnr])
    P.op("dve", I("tensor_tensor", out=ni, in0=RHO, in1=SA3[:, 4, :], op=ALU.mult), rd=[tRHO, tSA], wr=[tni])
    P.op("dve", I("tensor_tensor", out=dn, in0=are, in1=are, op=ALU.mult), rd=[tare], wr=[tdn])
    P.op("dve", I("tensor_tensor", out=t1, in0=aim, in1=aim, op=ALU.mult), rd=[taim], wr=[tt1])
    P.op("dve", I("tensor_tensor", out=dn, in0=dn, in1=t1, op=ALU.add), rd=[tdn, tt1], wr=[tdn])
    P.op("dve", I("reciprocal", out=dn, in_=dn), rd=[tdn], wr=[tdn])
    P.op("dve", I("tensor_tensor", out=KR, in0=nr, in1=are, op=ALU.mult), rd=[tnr, tare], wr=[tKR])
    P.op("dve", I("tensor_tensor", out=t1, in0=ni, in1=aim, op=ALU.mult), rd=[tni, taim], wr=[tt1])
    P.op("dve", I("tensor_tensor", out=KR, in0=KR, in1=t1, op=ALU.add), rd=[tKR, tt1], wr=[tKR])
    P.op("dve", I("tensor_tensor", out=KR, in0=KR, in1=dn, op=ALU.mult), rd=[tKR, tdn], wr=[tKR])
    P.op("dve", I("tensor_tensor", out=KI, in0=ni, in1=are, op=ALU.mult), rd=[tni, tare], wr=[tKI])
    P.op("dve", I("tensor_tensor", out=t1, in0=nr, in1=aim, op=ALU.mult), rd=[tnr, taim], wr=[tt1])
    P.op("dve", I("tensor_tensor", out=KI, in0=KI, in1=t1, op=ALU.subtract), rd=[tKI, tt1], wr=[tKI])
    P.op("dve", I("tensor_tensor", out=KI, in0=KI, in1=dn, op=ALU.mult), rd=[tKI, tdn], wr=[tKI])
    P.op("dve", I("tensor_scalar", out=KIN[0:64, :], in0=KI[0:64, :], scalar1=-1.0, scalar2=None, op0=ALU.mult), rd=[tKI], wr=[tKIN])
    P.op("dve", I("tensor_copy", out=KIN[64:128, :], in_=KI[64:128, :]), rd=[tKI], wr=[tKIN])
    P.op("dve", I("tensor_copy", out=KRN[0:64, :], in_=KR[0:64, :]), rd=[tKR], wr=[tKRN])
    P.op("dve", I("tensor_scalar", out=KRN[64:128, :], in0=KR[64:128, :], scalar1=-1.0, scalar2=None, op0=ALU.mult), rd=[tKR], wr=[tKRN])
    dsk, tdsk = load_vec(b, "s_d", prm["ssm_d"][l], SW)
    CT, tCT = mr.f32(NT, "ct")
    ST, tST = mr.f32(NT, "st")
    WW, tWW = mr.f32(2 * CH, "ww")
    WIN, WO = WW[:, 0:CH], WW[:, CH:2 * CH]
    tWIN, tWO = Tok("win", alias=[tWW]), Tok("wo", alias=[tWW])
    mr.toks += [tWIN, tWO]
    P1, tP1 = mr.bf16(CH, "p1")
    P2, tP2 = mr.bf16(CH, "p2")
    UB, tUB = mr.bf16(NT, "ub")
    RHOB, tRHOB = mr.f32(CH, "rhob")
    ZA, tZA = mr.f32(128, "za")
    ZB, tZB = mr.f32(128, "zb")
    X1, tX1 = mr.f32(16, "x1")
    X2, tX2 = mr.f32(16, "x2")
    CC1, tCC1 = mr.f32(128, "cc1")
    CC2, tCC2 = mr.f32(128, "cc2")
    LA, tLA = mr.bf16(128, "la")
    LB, tLB = mr.bf16(128, "lb")
    L1, tL1 = mr.bf16(128, "l1")
    L2, tL2 = mr.bf16(128, "l2")
    CARRY, tCARRY = mr.f32(1, "carry")
    YA, tYA = b.XT[:, 0:NT], b.T_XT
    for sg in range(SCH):
        ch0 = sg * 128
        dma(b, "sp", YA, b.uS[ch0:ch0 + 128, :], [b.T_uS], [tYA], tYA)
        P.op("act", I("activation", out=UB, in_=YA, func=AF.Copy), rd=[tYA], wr=[tUB])
        P.op("dve", I("tensor_scalar", out=YA, in0=YA, scalar1=dsk[:, sg:sg + 1], scalar2=None, op0=ALU.mult), rd=[tYA, tdsk, tUB], wr=[tYA])
        for gi in range(8):
            g = sg * 8 + gi
            P.op("pool", I("memset", CT[:, 0:1], 1.0), wr=[tCT])
            P.op("pool", I("memset", ST[:, 0:1], 0.0), wr=[tST])
            n = 1
            for k in range(LOG):
                ck, sk = CA3[:, 4 + k, g:g + 1], SA3[:, 4 + k, g:g + 1]
                tmp = WW[:, 0:n]
                P.op("dve", I("tensor_scalar", out=tmp, in0=ST[:, 0:n], scalar1=sk, scalar2=None, op0=ALU.mult), rd=[tST, tSA], wr=[tWW, tWIN, tWO])
                P.op("dve", I("scalar_tensor_tensor", out=CT[:, n:2 * n], in0=CT[:, 0:n], scalar=ck, in1=tmp, op0=ALU.mult, op1=ALU.subtract),
                     rd=[tCT, tCA, tWW], wr=[tCT])
                P.op("dve", I("tensor_scalar", out=tmp, in0=CT[:, 0:n], scalar1=sk, scalar2=None, op0=ALU.mult), rd=[tCT, tSA], wr=[tWW])
                P.op("dve", I("scalar_tensor_tensor", out=ST[:, n:2 * n], in0=ST[:, 0:n], scalar=ck, in1=tmp, op0=ALU.mult, op1=ALU.add),
                     rd=[tST, tCA, tWW], wr=[tST])
                n *= 2
            tWIN.lw, tWIN.rd, tWO.lw, tWO.rd = dict(tWW.lw), dict(tWW.rd), dict(tWW.lw), dict(tWW.rd)
            dma(b, "sp", X1[0:64, :], prm["ssm_b_re"][l][g], [], [tX1], tX1)
            dma(b, "sp", X1[64:128, :], prm["ssm_b_im"][l][g], [], [tX1], tX1)
            dma(b, "sp", X2[0:64, :], prm["ssm_b_im"][l][g], [], [tX2], tX2)
            dma(b, "sp", X2[64:128, :], prm["ssm_b_re"][l][g], [], [tX2], tX2)
            P.op("pool", I("memset", ZA, 0.0), wr=[tZA])
            P.op("pool", I("memset", ZB, 0.0), wr=[tZB])
            zs = slice(gi * 16, gi * 16 + 16)
            P.op("dve", I("tensor_scalar", out=ZA[:, zs], in0=X1, scalar1=KR[:, g:g + 1], scalar2=None, op0=ALU.mult), rd=[tX1, tKR], wr=[tZA])
            P.op("dve", I("scalar_tensor_tensor", out=ZA[:, zs], in0=X2, scalar=KIN[:, g:g + 1], in1=ZA[:, zs], op0=ALU.mult, op1=ALU.add),
                 rd=[tX2, tKIN, tZA], wr=[tZA])
            P.op("dve", I("tensor_scalar", out=ZB[:, zs], in0=X2, scalar1=KRN[:, g:g + 1], scalar2=None, op0=ALU.mult), rd=[tX2, tKRN], wr=[tZB])
            P.op("dve", I("scalar_tensor_tensor", out=ZB[:, zs], in0=X1, scalar=KI[:, g:g + 1], in1=ZB[:, zs], op0=ALU.mult, op1=ALU.add),
                 rd=[tX1, tKI, tZB], wr=[tZB])
            for Z, tZ, L, tL, bank in ((ZA, tZA, LA, tLA, 6), (ZB, tZB, LB, tLB, 7)):
                pz = b.psum[:, bank, 0:128]
                P.op("pe", I("transpose", out=pz, in_=Z, identity=b.ident_f), rd=[tZ, b.T_ident_f], wr=[b.PS[bank]])
                P.op("act", I("activation", out=L, in_=pz, func=AF.Copy), rd=[b.PS[bank]], wr=[tL])
            dma(b, "sp", CC1[0:16, 0:64], prm["ssm_c_re"][l][g], [], [tCC1], tCC1)
            dma(b, "sp", CC1[0:16, 64:128], prm["ssm_c_im"][l][g], [], [tCC1], tCC1)
            dma(b, "sp", CC2[0:16, 0:64], prm["ssm_c_im"][l][g], [], [tCC2], tCC2)
            dma(b, "sp", CC2[0:16, 64:128], prm["ssm_c_re"][l][g], [], [tCC2], tCC2)
            P.op("dve", I("tensor_scalar", out=CC1[0:16, 64:128], in0=CC1[0:16, 64:128], scalar1=-1.0, scalar2=None, op0=ALU.mult), rd=[tCC1], wr=[tCC1])
            P.op("dve", I("tensor_scalar", out=CC2[0:16, :], in0=CC2[0:16, :], scalar1=-1.0, scalar2=None, op0=ALU.mult), rd=[tCC2], wr=[tCC2])
            for CCx, tCx, L, tL, bank in ((CC1, tCC1, L1, tL1, 6), (CC2, tCC2, L2, tL2, 7)):
                pz = b.psum[:, bank, 0:16]
                P.op("pe", I("transpose", out=pz, in_=CCx[0:16, :], identity=b.ident_f[0:16, 0:16]), rd=[tCx, b.T_ident_f], wr=[b.PS[bank]])
                P.op("pool", I("memset", L, 0.0), wr=[tL])
                P.op("act", I("activation", out=L[:, zs], in_=pz, func=AF.Copy), rd=[b.PS[bank]], wr=[tL])
            P.op("act", I("activation", out=RHOB, in_=RHO[:, g:g + 1].to_broadcast([128, CH]), func=AF.Copy), rd=[tRHO], wr=[tRHOB])
            for ci in range(NT // CH):
                for blk in range(CH // 512):
                    t_0 = ci * CH + blk * 512
                    pA, tpA = nbank(b)
                    pB, tpB = nbank(b)
                    P.op("pe", I("matmul", pA, lhsT=LA, rhs=UB[:, t_0:t_0 + 512], start=True, stop=True), rd=[tLA, tUB], wr=[tpA])
                    P.op("pe", I("matmul", pB, lhsT=LB, rhs=UB[:, t_0:t_0 + 512], start=True, stop=True), rd=[tLB, tUB], wr=[tpB])
                    f1, tf1 = ft(b)
                    f2, tf2 = ft(b)
                    P.op("dve", I("tensor_tensor", out=f1, in0=pA, in1=CT[:, t_0:t_0 + 512], op=ALU.mult), rd=[tpA, tCT], wr=[tf1])
                    P.op("dve", I("tensor_tensor", out=f2, in0=pB, in1=ST[:, t_0:t_0 + 512], op=ALU.mult), rd=[tpB, tST], wr=[tf2])
                    P.op("pool", I("tensor_tensor", out=WIN[:, blk * 512:(blk + 1) * 512], in0=f1, in1=f2, op=ALU.add), rd=[tf1, tf2], wr=[tWIN])
                if ci == 0:
                    P.op("dve", I("tensor_tensor_scan", out=WO, data0=RHOB, data1=WIN, initial=0.0, op0=ALU.mult, op1=ALU.add),
                         rd=[tRHOB, tWIN], wr=[tWO])
                else:
                    P.op("dve", I("tensor_tensor_scan", out=WO, data0=RHOB, data1=WIN, initial=CARRY[:, 0:1], op0=ALU.mult, op1=ALU.add),
                         rd=[tRHOB, tWIN, tCARRY], wr=[tWO])
                P.op("act", I("activation", out=CARRY, in_=WO[:, CH - 1:CH], func=AF.Copy), rd=[tWO], wr=[tCARRY])
                P.op("pool", I("tensor_tensor", out=P1, in0=WO, in1=CT[:, ci * CH:(ci + 1) * CH], op=ALU.mult), rd=[tWO, tCT], wr=[tP1])
                P.op("dve", I("tensor_tensor", out=P2, in0=WO, in1=ST[:, ci * CH:(ci + 1) * CH], op=ALU.mult), rd=[tWO, tST], wr=[tP2])
                for blk in range(CH // 512):
                    t_0 = ci * CH + blk * 512
                    pY, tpY = nbank(b)
                    P.op("pe", I("matmul", pY, lhsT=L1, rhs=P1[:, blk * 512:(blk + 1) * 512], start=True, stop=False), rd=[tL1, tP1], wr=[tpY])
                    P.op("pe", I("matmul", pY, lhsT=L2, rhs=P2[:, blk * 512:(blk + 1) * 512], start=False, stop=True), rd=[tL2, tP2], wr=[tpY])
                    P.op("dve", I("tensor_tensor", out=YA[:, t_0:t_0 + 512], in0=pY, in1=YA[:, t_0:t_0 + 512], op=ALU.add), rd=[tpY, tYA], wr=[tYA])
        for t_0 in range(0, NT, 512):
            y = YA[:, t_0:t_0 + 512]
            a, ta = ft(b)
            s_, ts_ = ft(b)
            P.op("pool", I("tensor_tensor", out=a, in0=y, in1=y, op=ALU.mult), rd=[tYA], wr=[ta])
            P.op("dve", I("tensor_scalar", out=a, in0=a, scalar1=0.044715, scalar2=1.0, op0=ALU.mult, op1=ALU.add), rd=[ta], wr=[ta])
            P.op("pool", I("tensor_tensor", out=a, in0=a, in1=y, op=ALU.mult), rd=[ta, tYA], wr=[ta])
            P.op("act", I("activation", out=s_, in_=a, func=AF.Sigmoid, scale=1.5957691216), rd=[ta], wr=[ts_])
            P.op("dve", I("tensor_tensor", out=a, in0=y, in1=s_, op=ALU.mult), rd=[tYA, ts_, ta], wr=[ta])
            dma(b, "sp", b.zS[ch0:ch0 + 128, t_0:t_0 + 512], a, [ta], [b.T_zS], b.T_zS)
    mr.close()
    mr = MR(b)
    WG, tWG = mr.bf16(SCH * SW, "wg")
    WG3 = WG.rearrange("p (k n) -> p k n", k=SCH)
    dma(b, "pool", WG3, prm["ssm_w_glu"][l].rearrange("(k p) n -> p k n", p=128), [], [tWG], tWG)
    ZF, tZFl = mr.f32(SCH * 512, "zf")
    ZF3 = ZF.rearrange("p (k t) -> p k t", k=SCH)
    tZF = [Tok("zf%d" % i, alias=[tZFl]) for i in range(SCH)]
    mr.toks += tZF
    ZBf, tZB_ = mr.bf16(SCH * 512, "zbf")
    ZB3 = ZBf.rearrange("p (k t) -> p k t", k=SCH)
    R, tR = mr.f32(512, "gr")
    bg, tbg = load_vec(b, "s_bg", prm["ssm_b_glu"][l], SW)
    for t_0 in range(0, NT, 512):
        dma(b, "sp", ZF3, b.zS[:, t_0:t_0 + 512].rearrange("(k p) t -> p k t", p=128), [b.T_zS], tZF + [tZFl], tZFl)
        P.op("act", I("activation", out=ZBf, in_=ZF, func=AF.Copy), rd=tZF, wr=[tZB_])
        S3, tS3 = b.psum[:, 6, 0:512], b.PS[6]
        for oc in range(SCH):
            acc, tacc = nbank(b)
            for kc in range(SCH):
                P.op("pe", I("matmul", acc, lhsT=WG3[:, kc, oc * 128:(oc + 1) * 128], rhs=ZB3[:, kc, :], start=(kc == 0), stop=(kc == SCH - 1)),
                     rd=[tWG, tZB_], wr=[tacc])
            sg_, tsg_ = ft(b)
            P.op("act", I("activation", out=sg_, in_=acc, func=AF.Sigmoid, bias=bg[:, oc:oc + 1]), rd=[tacc, tbg], wr=[tsg_])
            P.op("pool", I("tensor_tensor", out=ZF3[:, oc, :], in0=ZF3[:, oc, :], in1=sg_, op=ALU.mult), rd=[tZF[oc], tsg_], wr=[tZF[oc]])
            sumsq_acc(b, ZF3[:, oc, :], tZF[oc], S3, tS3, oc == 0, oc == SCH - 1, 512)
        group_rms_store(b, [(ZF3[:, oc, :], tZF[oc]) for oc in range(SCH)], S3, tS3, R, tR, SW, go, (AW + CW) // 128, tgo, AW + CW, t_0, 512)
    mr.close()
```

```python
import math
import numpy as np
import concourse.bass as bass
import concourse.mybir as mybir
from concourse.bass_utils import run_bass_kernel_spmd

F32 = mybir.dt.float32
BF16 = mybir.dt.bfloat16
I32 = mybir.dt.int32
AF = mybir.ActivationFunctionType
ALU = mybir.AluOpType
AX = mybir.AxisListType

COMPUTE = ("pe", "act", "dve", "pool")
ENGS = ("pe", "act", "dve", "pool", "sp")
SAME_DIST = 3


def I(meth, *a, **k):
    return lambda e: getattr(e, meth)(*a, **k)


class Tok:
    __slots__ = ("name", "lw", "rd", "sem", "cnt", "id")
    _n = 0

    def __init__(self, name, alias=()):
        self.name = name
        self.lw = {}
        self.rd = {}
        self.sem = None
        self.cnt = 0
        Tok._n += 1
        self.id = Tok._n
        for a in alias:
            for src, dst in ((a.lw, self.lw), (a.rd, self.rd)):
                for k, p in src.items():
                    if k not in dst or dst[k].seq < p.seq:
                        dst[k] = p


class Op:
    __slots__ = ("eng", "fn", "waits", "signal", "clock", "seq", "snap", "dtok", "sigval")


class Prog:
    def __init__(self):
        self.ops = {e: [] for e in ENGS}
        self.know = {e: {} for e in ENGS}
        self.nseq = {e: 0 for e in ENGS}
        self.dma_toks = []
        self.semcnt = {}
        self.lsnap = {e: {} for e in ENGS}

    def op(self, eng, fn, rd=(), wr=(), dma=None, skip=()):
        o = Op()
        o.eng = eng
        o.fn = fn
        o.waits = []
        o.signal = False
        o.dtok = dma
        o.sigval = None
        know = self.know[eng]
        if dma is None:
            self.nseq[eng] += 1
            o.clock = eng
            o.seq = self.nseq[eng]
        else:
            sk = dma.name
            if sk not in self.semcnt:
                self.semcnt[sk] = 0
                self.dma_toks.append(sk)
            self.semcnt[sk] += 1
            o.clock = ("d", sk)
            o.seq = self.semcnt[sk]
        deps = []
        for t in rd:
            deps += t.lw.values()
        for t in wr:
            deps += t.lw.values()
            deps += t.rd.values()
        myseq = self.nseq[eng]
        dirty = False
        for p in deps:
            if p is o or p in skip:
                continue
            c = p.clock
            if c == eng:
                if eng == "pe" or eng == "sp":
                    continue
                if myseq - p.seq > SAME_DIST or know.get(c, 0) >= p.seq:
                    continue
            elif know.get(c, 0) >= p.seq:
                continue
            o.waits.append(p)
            p.signal = True
            dirty = True
            for k, v in p.snap.items():
                if know.get(k, 0) < v:
                    know[k] = v
            if know.get(c, 0) < p.seq:
                know[c] = p.seq
        if dirty:
            self.lsnap[eng] = dict(know)
        o.snap = self.lsnap[eng]
        for t in wr:
            t.lw = {o.clock: o}
            t.rd = {}
        for t in rd:
            t.rd[o.clock] = o
        self.ops[eng].append(o)
        return o

    def emit(self, nc, stack):
        sems = {}
        for e in COMPUTE:
            sems[e] = stack.enter_context(nc.semaphore("s_" + e))
        for i, sk in enumerate(self.dma_toks):
            sems[("d", sk)] = stack.enter_context(nc.semaphore("d%d" % i))
        for e in COMPUTE:
            n = 0
            for o in self.ops[e]:
                if o.dtok is None and o.signal:
                    n += 1
                    o.sigval = n
        block = stack.enter_context(nc.Block())

        def run(eng_name):
            def body(eng):
                for o in self.ops[eng_name]:
                    for p in o.waits:
                        if p.dtok is None:
                            eng.wait_ge(sems[p.clock], p.sigval)
                        else:
                            eng.wait_ge(sems[p.clock], 16 * p.seq)
                    if o.fn is None:
                        continue
                    ins = o.fn(eng)
                    if o.dtok is not None:
                        ins.then_inc(sems[o.clock], 16)
                    elif o.signal:
                        ins.then_inc(sems[o.clock], 1)
            return body

        block.tensor(run("pe"))
        block.scalar(run("act"))
        block.vector(run("dve"))
        block.gpsimd(run("pool"))
        block.sync(run("sp"))


class Cfg:
    def __init__(self, D=4096, NT=4096, depth=2, mem_len=256):
        self.D = D
        self.NT = NT
        self.depth = depth
        self.KC = D // 128
        self.TG = min(1024, NT)
        self.NG = NT // self.TG
        self.AW = D // 2
        self.H = self.AW // 128
        self.CW = D // 4
        self.CCH = self.CW // 128
        self.SW = D // 4
        self.SCH = self.SW // 128
        self.INW = 3 * self.AW + 2 * self.CW + self.SW
        self.FF = 4 * D
        self.ML = mem_len
        self.MW = 512
        self.NTT = NT // 128


PARAM_SPECS = None


def param_shapes(c):
    D = c.D
    return {
        "norm_mix": (D,), "w_in": (D, c.INW), "q_norm": (128,), "k_norm": (128,),
        "conv_dw": (31, c.CW), "conv_b": (c.CW,), "conv_ln_g": (c.CW,), "conv_ln_b": (c.CW,),
        "ssm_a_re": (c.SW // 16, 64), "ssm_a_im": (c.SW // 16, 64),
        "ssm_b_re": (c.SW // 16, 64, 16), "ssm_b_im": (c.SW // 16, 64, 16),
        "ssm_c_re": (c.SW // 16, 16, 64), "ssm_c_im": (c.SW // 16, 16, 64),
        "ssm_d": (c.SW,), "ssm_log_step": (c.SW // 16,), "ssm_w_glu": (c.SW, c.SW), "ssm_b_glu": (c.SW,),
        "mix_out_norm": (D,), "w_out": (D, D), "norm_cross": (D,), "norm_mem": (D,),
        "w_cq": (D, 512), "w_ckv": (D, 1024), "cq_norm": (128,), "ck_norm": (128,), "w_co": (512, D),
        "norm_mlp": (D,), "w_up": (D, c.FF), "w_down": (c.FF, D),
    }


class B:
    pass


def _setup(c, dbg):
    from contextlib import ExitStack
    b = B()
    b.c = c
    b.dbg = dbg
    b.stack = ExitStack()
    nc = bass.Bass("TRN2", target_bir_lowering=False)
    b.nc = nc
    b.P = Prog()
    b.x_in = nc.dram_tensor("x", [c.NT, c.D], F32, kind="ExternalInput").ap()
    b.mem_in = nc.dram_tensor("mem", [c.ML, c.D], F32, kind="ExternalInput").ap()
    b.prm = {}
    for n, s in param_shapes(c).items():
        b.prm[n] = nc.dram_tensor(n, [c.depth] + list(s), F32, kind="ExternalInput").ap()
    b.cst = {}
    for n, s in const_shapes(c).items():
        b.cst[n] = nc.dram_tensor(n, list(s), F32, kind="ExternalInput").ap()
    b.out = nc.dram_tensor("out", [c.NT, c.D], F32, kind="ExternalOutput").ap()
    b.xr = nc.dram_tensor("xr", [c.NT, c.D], F32).ap()
    xr8 = [Tok("xr%d" % i) for i in range(8)]
    b.XR = [xr8[i % 8] for i in range(c.NTT)]
    AR_F = 47872
    b.arena = b.stack.enter_context(nc.sbuf_tensor("arena", [128, AR_F], F32))
    b.small = b.stack.enter_context(nc.sbuf_tensor("small", [128, 4096], F32))
    b.psum = b.stack.enter_context(nc.psum_tensor("ps", [128, 8, 512], F32))
    b.PS = [Tok("ps%d" % i) for i in range(8)]
    b.soff = 0
    b.T_memin = Tok("memin")
    return b


def const_shapes(c):
    return {"ident": (128, 128), "amask": (17, 128, 128), "iota": (128, 4096)}


def salloc(b, n):
    o = b.soff
    b.soff += n
    assert b.soff <= 4096
    return b.small[:, o:o + n]


def dma(b, q, out, in_, rd, wr, tok, **kw):
    if len(out.shape) == 3 and out.shape[0] * out.shape[1] > 1024 and len(in_.shape) == 3:
        o = None
        prev = []
        for k0 in range(0, out.shape[1], 8):
            k1 = min(out.shape[1], k0 + 8)
            o = b.P.op(q, I("dma_start", out=out[:, k0:k1, :], in_=in_[:, k0:k1, :], **kw), rd=rd, wr=wr, dma=tok, skip=prev)
            prev.append(o)
        return o
    return b.P.op(q, I("dma_start", out=out, in_=in_, **kw), rd=rd, wr=wr, dma=tok)


def load_consts(b):
    c, P = b.c, b.P
    b.ident_f = salloc(b, 128)
    b.T_ident_f = Tok("identf")
    dma(b, "sp", b.ident_f, b.cst["ident"], [], [b.T_ident_f], b.T_ident_f)
    idb = salloc(b, 64)
    b.ident_b = idb.bitcast(BF16)
    b.T_ident_b = Tok("identb")
    P.op("dve", lambda e: e.tensor_copy(out=b.ident_b, in_=b.ident_f), rd=[b.T_ident_f], wr=[b.T_ident_b])
    b.ones_f = salloc(b, 128)
    b.T_ones = Tok("ones")
    P.op("pool", lambda e: e.memset(b.ones_f, 1.0), wr=[b.T_ones])
    onb = salloc(b, 64)
    b.ones_b = onb.bitcast(BF16)
    P.op("pool", lambda e: e.memset(b.ones_b, 1.0), wr=[b.T_ones])


def load_vec_pm(b, src_row, n, name):
    P = b.P
    k = n // 128
    dst = salloc(b, k)
    stg = salloc(b, 128)
    t_s, t_d = Tok(name + "_s"), Tok(name)
    dma(b, "sp", stg[0:k, :], src_row.rearrange("(k p) -> k p", p=128), [], [t_s], t_s)
    pst, PT = b.psum[:, 7, 0:k], b.PS[7]
    P.op("pe", lambda e: e.transpose(out=pst, in_=stg[0:k, :], identity=b.ident_f[0:k, 0:k]), rd=[t_s, b.T_ident_f], wr=[PT])
    P.op("dve", lambda e: e.tensor_copy(out=dst, in_=pst), rd=[PT], wr=[t_d])
    return dst, t_d


def norm_to_AT(b, src, src_toks, ntile, D, gain, gain_tok, AT, AT_toks, XT, HB, T_XT, T_HB, ssb, T_ss, eps=1e-6):
    P, c = b.P, b.c
    KC = D // 128
    for t in range(ntile):
        dma(b, "sp", XT[:, 0:D], src[t * 128:(t + 1) * 128, :], [src_toks[t]], [T_XT], T_XT)
        P.op("dve", lambda e: e.scalar_tensor_tensor(out=HB[:, 0:D], in0=XT[:, 0:D], scalar=1.0, in1=XT[:, 0:D],
                                                      op0=ALU.mult, op1=ALU.mult, accum_out=ssb[:, 0:1]),
             rd=[T_XT], wr=[T_HB, T_ss])
        P.op("act", lambda e: e.activation(out=ssb[:, 1:2], in_=ssb[:, 0:1], func=AF.Sqrt, scale=1.0 / D, bias=eps),
             rd=[T_ss], wr=[T_ss])
        P.op("dve", lambda e: e.reciprocal(out=ssb[:, 2:3], in_=ssb[:, 1:2]), rd=[T_ss], wr=[T_ss])
        P.op("act", lambda e: e.activation(out=HB[:, 0:D], in_=XT[:, 0:D], func=AF.Copy, scale=ssb[:, 2:3]),
             rd=[T_XT, T_ss], wr=[T_HB])
        for k0 in range(0, KC, 4):
            n = min(4, KC - k0)
            bank = 4 + (k0 // 4) % 2
            pst = b.psum[:, bank, 0:256].bitcast(BF16)[:, 0:n * 128]
            for j in range(n):
                P.op("pe", lambda e, j=j, k0=k0, pst=pst: e.transpose(out=pst[:, j * 128:(j + 1) * 128],
                                                                      in_=HB[:, (k0 + j) * 128:(k0 + j + 1) * 128],
                                                                      identity=b.ident_b),
                     rd=[T_HB, b.T_ident_b], wr=[b.PS[bank]])
            gb = gain[:, k0:k0 + n].unsqueeze(2).to_broadcast([128, n, 128])
            P.op("dve", lambda e, k0=k0, n=n, pst=pst, gb=gb, t=t: e.tensor_tensor(
                out=AT[:, k0:k0 + n, t * 128:(t + 1) * 128], in0=pst.rearrange("p (k t) -> p k t", k=n), in1=gb, op=ALU.mult),
                rd=[b.PS[bank], gain_tok], wr=[AT_toks[t]])


def gemm(b, A, A_toks, KCa, ntok, W, cols, mode, epi, Wt, T_Wt, wq="pool"):
    P = b.P
    Wv = W.rearrange("(k p) n -> p k n", p=128)
    def issue(i):
        c0_, w_ = cols[i]
        s_ = i % len(Wt)
        dma(b, wq, Wt[s_][:, 0:KCa, 0:w_], Wv[:, :, c0_:c0_ + w_], [], [T_Wt[s_]], T_Wt[s_])

    issue(0)
    for i, (c0, w) in enumerate(cols):
        s = i % len(Wt)
        wt, tw = Wt[s], T_Wt[s]
        if i + 1 < len(cols):
            issue(i + 1)
        if mode == "FM":
            for cc in range(0, w, 128):
                for tb in range(0, ntok, 512):
                    nb = min(512, ntok - tb)
                    bank = b.gbank
                    b.gbank = (b.gbank + 1) % 4
                    acc = b.psum[:, bank, 0:nb]
                    for kc in range(KCa):
                        P.op("pe", lambda e, acc=acc, wt=wt, kc=kc, cc=cc, tb=tb, nb=nb: e.matmul(
                            acc, lhsT=wt[:, kc, cc:cc + 128], rhs=A[:, kc, tb:tb + nb], start=(kc == 0), stop=(kc == KCa - 1)),
                            rd=[tw] + A_toks[tb // 128:(tb + nb) // 128], wr=[b.PS[bank]])
                    epi(acc, b.PS[bank], c0 + cc, tb, nb)
        else:
            for tt in range(ntok // 128):
                bank = b.gbank
                b.gbank = (b.gbank + 1) % 4
                acc = b.psum[:, bank, 0:w]
                for kc in range(KCa):
                    P.op("pe", lambda e, acc=acc, wt=wt, kc=kc, tt=tt, w=w: e.matmul(
                        acc, lhsT=A[:, kc, tt * 128:(tt + 1) * 128], rhs=wt[:, kc, 0:w], start=(kc == 0), stop=(kc == KCa - 1)),
                        rd=[tw, A_toks[tt]], wr=[b.PS[bank]])
                epi(acc, b.PS[bank], c0, w, tt)


def colblocks(n0, n1, w=256):
    return [(c0, min(w, n1 - c0)) for c0 in range(n0, n1, w)]


def rmw_epi(b, tile0):
    P = b.P

    def epi(acc, tacc, c0, w, tt):
        i = b.rmw_i
        b.rmw_i = (i + 1) % len(b.RT)
        rt, trt = b.RT[i], b.T_RT[i]
        tx = b.XR[tile0 + tt]
        rows = b.xr[(tile0 + tt) * 128:(tile0 + tt + 1) * 128, c0:c0 + w]
        dma(b, "sp", rt[:, 0:w], rows, [tx], [trt], trt)
        P.op("dve", lambda e: e.tensor_tensor(out=rt[:, 0:w], in0=acc, in1=rt[:, 0:w], op=ALU.add), rd=[tacc, trt], wr=[trt])
        dma(b, "sp", rows, rt[:, 0:w], [trt], [tx], tx)
    return epi


def carve(b):
    c = b.c
    ar = b.arena
    KC, TG = c.KC, c.TG
    b.A0 = ar[:, 0:16384].bitcast(BF16)[:, 0:KC * TG].rearrange("p (k t) -> p k t", k=KC)
    b.A1 = ar[:, 16384:32768].bitcast(BF16)[:, 0:KC * TG].rearrange("p (k t) -> p k t", k=KC)
    b.Wt = [ar[:, 32768 + i * 4096:32768 + (i + 1) * 4096].bitcast(BF16).rearrange("p (k n) -> p k n", n=256) for i in range(2)]
    b.XT = ar[:, 40960:45056]
    b.HB = ar[:, 45056:47104].bitcast(BF16)
    b.RT = [ar[:, 47104 + i * 256:47104 + (i + 1) * 256] for i in range(3)]
    ntg = TG // 128
    b.T_A0 = [Tok("a0_%d" % i) for i in range(ntg)]
    b.T_A1 = [Tok("a1_%d" % i) for i in range(ntg)]
    b.T_Wt = [Tok("wt%d" % i) for i in range(2)]
    b.T_XT, b.T_HB = Tok("xt"), Tok("hb")
    b.T_RT = [Tok("rt%d" % i) for i in range(3)]
    b.ssb = salloc(b, 4)
    b.T_ss = Tok("ss")
    b.gbank = 0
    b.rmw_i = 0
    b.FT = [salloc(b, 512) for i in range(3)]
    b.T_FT = [Tok("ft%d" % i) for i in range(3)]
    b.PTt = [salloc(b, 256).bitcast(BF16) for i in range(2)]
    b.T_PTt = [Tok("pt%d" % i) for i in range(2)]
    b.ft_i = 0


def mlp_phase(b, l, g, gain, gain_tok):
    c, P = b.c, b.P
    ntg = c.TG // 128
    t0 = g * ntg
    norm_to_AT(b, b.xr[g * c.TG:(g + 1) * c.TG, :], b.XR[t0:t0 + ntg], ntg, c.D, gain, gain_tok,
               b.A0, b.T_A0, b.XT, b.HB, b.T_XT, b.T_HB, b.ssb, b.T_ss)
    D = c.D

    def relu2_epi(acc, tacc, col, tb, nb):
        i = b.ft_i
        b.ft_i = (i + 1) % len(b.FT)
        ft, tft = b.FT[i], b.T_FT[i]
        j = (col % D) // 128
        P.op("act", lambda e: e.activation(out=ft[:, 0:nb], in_=acc, func=AF.Relu), rd=[tacc], wr=[tft])
        P.op("pool", lambda e: e.tensor_tensor(out=b.A1[:, j, tb:tb + nb], in0=ft[:, 0:nb], in1=ft[:, 0:nb], op=ALU.mult),
             rd=[tft], wr=b.T_A1[tb // 128:(tb + nb) // 128])

    for q in range(4):
        gemm(b, b.A0, b.T_A0, c.KC, c.TG, b.prm["w_up"][l], colblocks(q * D, (q + 1) * D), "FM", relu2_epi, b.Wt, b.T_Wt)
        gemm(b, b.A1, b.T_A1, c.KC, c.TG, b.prm["w_down"][l][q * D:(q + 1) * D, :], colblocks(0, D), "TM",
             rmw_epi(b, t0), b.Wt, b.T_Wt)


def copy_rows(b, dst, dst_toks, src, src_toks, ntile):
    for t in range(ntile):
        dma(b, "sp", dst[t * 128:(t + 1) * 128, :], src[t * 128:(t + 1) * 128, :],
            [src_toks[t]] if src_toks else [], [dst_toks[t]], dst_toks[t])


def build(c, phases=("mix", "attn", "conv", "ssm", "cross", "mlp"), dbg=()):
    b = _setup(c, dbg)
    P = b.P
    load_consts(b)
    carve(b)
    b.T_OUT = [Tok("out")] * c.NTT
    copy_rows(b, b.xr, b.XR, b.x_in, None, c.NTT)
    for l in range(c.depth):
        soff0 = b.soff
        if "mix" in phases:
            mix_setup(b)
            gx, tgx = load_vec(b, "g_mix", b.prm["norm_mix"][l], c.D)
            go, tgo = load_vec(b, "g_mo", b.prm["mix_out_norm"][l], c.D)
            qn, tqn = load_vec(b, "g_q", b.prm["q_norm"][l], 128)
            kn, tkn = load_vec(b, "g_k", b.prm["k_norm"][l], 128)
            qns, tqns = palloc(b, "g_qs", 1)
            P.op("dve", I("tensor_scalar", out=qns, in0=qn, scalar1=128.0 ** -0.5, scalar2=None, op0=ALU.mult), rd=[tqn], wr=[tqns])
            for g in range(c.NG):
                inproj_phase(b, l, g, gx, tgx, qns, tqns, kn, tkn)
            if "attn" in phases:
                attn_phase(b, l)
                attn_norm_phase(b, l, go, tgo)
            if "conv" in phases:
                conv_phase(b, l, go, tgo)
            if "ssm" in phases:
                ssm_phase(b, l, go, tgo)
            if not all(p in phases for p in ("attn", "conv", "ssm")):
                zt = b.HB[:, 0:c.NT]
                P.op("pool", I("memset", zt, 0.0), wr=[b.T_HB])
                for nm, r0, r1 in (("attn", 0, c.AW), ("conv", c.AW, c.AW + c.CW), ("ssm", c.AW + c.CW, c.D)):
                    if nm not in phases:
                        for r in range(r0, r1, 128):
                            dma(b, "sp", b.mixT[r:r + 128, :], zt, [b.T_HB], [b.T_mixT], b.T_mixT)
            for g in range(c.NG):
                wout_phase(b, l, g)
        if "cross" in phases:
            cross_phase_prologue(b, l)
            gc, tgc = load_vec(b, "g_cross", b.prm["norm_cross"][l], c.D)
            cq, tcq = load_vec(b, "g_cq", b.prm["cq_norm"][l], 128)
            cqs, tcqs = palloc(b, "g_cqs", 1)
            P.op("dve", lambda e: e.tensor_scalar(out=cqs, in0=cq, scalar1=128.0 ** -0.5, scalar2=None, op0=ALU.mult), rd=[tcq], wr=[tcqs])
            for g in range(c.NG):
                cross_phase(b, l, g, gc, tgc, cqs, tcqs)
        if "mlp" in phases:
            gm, tgm = load_vec(b, "g_mlp", b.prm["norm_mlp"][l], c.D)
            for g in range(c.NG):
                mlp_phase(b, l, g, gm, tgm)
    copy_rows(b, b.out, b.T_OUT, b.xr, b.XR, c.NTT)
    P.op("sp", None, rd=b.T_OUT)
    P.emit(b.nc, b.stack)
    b.stack.close()
    return b.nc


def make_consts(c):
    ident = np.eye(128, dtype=np.float32)
    i = np.arange(128)
    am = np.zeros((17, 128, 128), np.float32)
    for dl in range(17):
        diff = 128 * dl + i[None, :] - i[:, None]
        for d in (1, 4, 16):
            am[dl] += ((diff >= 0) & (diff % d == 0) & (diff // d <= 128)).astype(np.float32)
    iota = np.broadcast_to(np.arange(4096, dtype=np.float32), (128, 4096)).copy()
    return {"ident": ident, "amask": am, "iota": iota}


_CACHE = {}


def kernel(**inputs):
    c = Cfg()
    x = np.asarray(inputs["x"], dtype=np.float32)
    nb = x.shape[0]
    if "nc" not in _CACHE:
        _CACHE["nc"] = build(c)
    nc = _CACHE["nc"]
    consts = make_consts(c)
    shp = param_shapes(c)
    shared = {n: np.ascontiguousarray(np.asarray(inputs[n], dtype=np.float32).reshape([c.depth] + list(shp[n]))) for n in shp}
    shared.update(consts)
    in_maps = []
    for i in range(nb):
        m = dict(shared)
        m["x"] = np.ascontiguousarray(x[i])
        m["mem"] = np.ascontiguousarray(np.asarray(inputs["mem"], dtype=np.float32)[i])
        in_maps.append(m)
    res = run_bass_kernel_spmd(nc, in_maps, core_ids=list(range(nb)))
    return np.stack([np.asarray(r["out"], dtype=np.float32) for r in res.results], axis=0)


def palloc(b, name, n):
    d = b.__dict__.setdefault("_pa", {})
    if name not in d:
        d[name] = (salloc(b, n), Tok(name))
    return d[name]


def load_mat_pm(b, name, src, r, off=0):
    P = b.P
    dst, t_d = palloc(b, name, r)
    stg, t_s = palloc(b, "stg", 128)
    dma(b, "sp", stg[0:r, :], src, [], [t_s], t_s)
    pst, PT = b.psum[:, 7, 0:r], b.PS[7]
    P.op("pe", lambda e: e.transpose(out=pst, in_=stg[0:r, :], identity=b.ident_f[0:r, 0:r]), rd=[t_s, b.T_ident_f], wr=[PT])
    P.op("dve", lambda e: e.tensor_copy(out=dst, in_=pst), rd=[PT], wr=[t_d])
    return dst, t_d


def load_vec(b, name, src_row, n):
    return load_mat_pm(b, name, src_row.rearrange("(k p) -> k p", p=128), n // 128)


def ft(b):
    i = b.ft_i
    b.ft_i = (i + 1) % len(b.FT)
    return b.FT[i], b.T_FT[i]


def fm_rms_scale(b, acc, tacc, nb, gain_col, gain_tok, scale, out, out_toks, n_feat=128):
    P = b.P
    sq, tsq = ft(b)
    P.op("act", lambda e: e.activation(out=sq[:, 0:nb], in_=acc, func=AF.Square), rd=[tacc], wr=[tsq])
    ssp, tss = b.psum[:, 6, 0:nb], b.PS[6]
    P.op("pe", lambda e: e.matmul(ssp, lhsT=b.ones_f, rhs=sq[:, 0:nb], start=True, stop=True), rd=[tsq, b.T_ones], wr=[tss])
    P.op("act", lambda e: e.activation(out=sq[:, 0:nb], in_=ssp, func=AF.Sqrt, scale=1.0 / n_feat, bias=1e-6), rd=[tss], wr=[tsq])
    P.op("dve", lambda e: e.reciprocal(out=sq[:, 0:nb], in_=sq[:, 0:nb]), rd=[tsq], wr=[tsq])
    P.op("dve", lambda e: e.tensor_tensor(out=sq[:, 0:nb], in0=acc, in1=sq[:, 0:nb], op=ALU.mult), rd=[tacc, tsq], wr=[tsq])
    P.op("act", lambda e: e.activation(out=out, in_=sq[:, 0:nb], func=AF.Copy, scale=gain_col), rd=[tsq, gain_tok], wr=out_toks)
    if scale != 1.0:
        raise NotImplementedError


def cross_phase_prologue(b, l):
    c, P = b.c, b.P
    gm, tgm = load_vec(b, "g_mem", b.prm["norm_mem"][l], c.D)
    ck, tck = load_vec(b, "g_ck", b.prm["ck_norm"][l], 128)
    nmt = c.ML // 128
    norm_to_AT(b, b.mem_in, [b.T_memin] * nmt, nmt, c.D, gm, tgm, b.A1, b.T_A1, b.XT, b.HB, b.T_XT, b.T_HB, b.ssb, b.T_ss)
    mk, tmk = palloc(b, "memK", 4 * c.ML // 2)
    mv, tmv = palloc(b, "memV", nmt * 512 // 2)
    b.memK = mk.bitcast(BF16).rearrange("p (h m) -> p h m", h=4)
    b.memV = mv.bitcast(BF16).rearrange("p (t d) -> p t d", t=nmt)
    b.T_memK, b.T_memV = tmk, tmv

    def k_epi(acc, tacc, col, tb, nb):
        h = col // 128
        fm_rms_scale(b, acc, tacc, nb, ck[:, 0:1], tck, 1.0, b.memK[:, h, tb:tb + nb], [tmk])

    def v_epi(acc, tacc, c0, w, tt):
        P.op("act", lambda e: e.activation(out=b.memV[:, tt, c0 - 512:c0 - 512 + w], in_=acc, func=AF.Copy), rd=[tacc], wr=[tmv])

    gemm(b, b.A1, b.T_A1, c.KC, c.ML, b.prm["w_ckv"][l], colblocks(0, 512), "FM", k_epi, b.Wt, b.T_Wt)
    gemm(b, b.A1, b.T_A1, c.KC, c.ML, b.prm["w_ckv"][l], colblocks(512, 1024), "TM", v_epi, b.Wt, b.T_Wt)


def cross_phase(b, l, g, gain, gain_tok, cq, tcq):
    c, P = b.c, b.P
    ntg = c.TG // 128
    t0 = g * ntg
    TG = c.TG
    norm_to_AT(b, b.xr[g * TG:(g + 1) * TG, :], b.XR[t0:t0 + ntg], ntg, c.D, gain, gain_tok,
               b.A0, b.T_A0, b.XT, b.HB, b.T_XT, b.T_HB, b.ssb, b.T_ss)
    qc = b.HB[:, 0:4 * TG].rearrange("p (h t) -> p h t", h=4)

    def q_epi(acc, tacc, col, tb, nb):
        h = col // 128
        fm_rms_scale(b, acc, tacc, nb, cq[:, 0:1], tcq, 1.0, qc[:, h, tb:tb + nb], [b.T_HB])

    gemm(b, b.A0, b.T_A0, c.KC, TG, b.prm["w_cq"][l], colblocks(0, 512), "FM", q_epi, b.Wt, b.T_Wt)
    nmt = c.ML // 128
    for h in range(4):
        for tb in range(0, TG, 512):
            pts = []
            for mt in range(nmt):
                bank = b.gbank
                b.gbank = (b.gbank + 1) % 4
                sc = b.psum[:, bank, 0:512]
                P.op("pe", I("matmul", sc, lhsT=b.memK[:, h, mt * 128:(mt + 1) * 128], rhs=qc[:, h, tb:tb + 512],
                             start=True, stop=True), rd=[b.T_memK, b.T_HB], wr=[b.PS[bank]])
                pt, tpt = b.PTt[mt % 2], b.T_PTt[mt % 2]
                P.op("act", I("activation", out=pt, in_=sc, func=AF.Exp), rd=[b.PS[bank]], wr=[tpt])
                pts.append((pt, tpt))
            den, tden = b.psum[:, 6, 0:512], b.PS[6]
            num, tnum = b.psum[:, 7, 0:512], b.PS[7]
            for mt in range(nmt):
                pt, tpt = pts[mt]
                P.op("pe", I("matmul", den, lhsT=b.ones_b, rhs=pt, start=(mt == 0), stop=(mt == nmt - 1)),
                     rd=[tpt, b.T_ones], wr=[tden])
            for mt in range(nmt):
                pt, tpt = pts[mt]
                P.op("pe", I("matmul", num, lhsT=b.memV[:, mt, h * 128:(h + 1) * 128], rhs=pt,
                             start=(mt == 0), stop=(mt == nmt - 1)), rd=[tpt, b.T_memV], wr=[tnum])
            rd_, trd = ft(b)
            P.op("dve", I("reciprocal", out=rd_, in_=den), rd=[tden], wr=[trd])
            P.op("dve", I("tensor_tensor", out=b.A1[:, h, tb:tb + 512], in0=num, in1=rd_, op=ALU.mult),
                 rd=[tnum, trd], wr=b.T_A1[tb // 128:tb // 128 + 4])
    gemm(b, b.A1, b.T_A1, 4, TG, b.prm["w_co"][l], colblocks(0, c.D), "TM", rmw_epi(b, t0), b.Wt, b.T_Wt)


def mix_setup(b):
    c, nc = b.c, b.nc
    if hasattr(b, "qT"):
        return
    b.qT = nc.dram_tensor("qT", [c.H, 128, c.NT], BF16).ap()
    b.kT = nc.dram_tensor("kT", [c.H, 128, c.NT], BF16).ap()
    b.vS = nc.dram_tensor("vS", [c.NT, c.AW], BF16).ap()
    b.hcS = nc.dram_tensor("hcS", [c.CW, c.NT], F32).ap()
    b.uS = nc.dram_tensor("uS", [c.SW, c.NT], F32).ap()
    b.attS = nc.dram_tensor("attS", [c.NT, c.AW], F32).ap()
    b.zS = nc.dram_tensor("zS", [c.SW, c.NT], F32).ap()
    b.mixT = nc.dram_tensor("mixT", [c.D, c.NT], BF16).ap()
    for n in ("qT", "kT", "vS", "hcS", "uS", "attS", "zS", "mixT"):
        setattr(b, "T_" + n, Tok(n))


class MR:
    def __init__(self, b, lo=16384):
        self.b = b
        self.off = lo
        self.lo = lo
        self.toks = []
        self.al = (list(b.T_A0) if lo == 0 else []) + list(b.T_A1)

    def f32(self, n, name):
        o = self.off
        self.off += n
        assert self.off <= 32768, name
        t = Tok(name, alias=self.al)
        self.toks.append(t)
        return self.b.arena[:, o:o + n], t

    def bf16(self, n, name):
        a, t = self.f32((n + 1) // 2, name)
        return a.bitcast(BF16)[:, 0:n], t

    def close(self):
        self.b.T_A1 = [Tok("a1_%d" % i, alias=self.toks + self.b.T_A1) for i in range(len(self.b.T_A1))]
        if self.lo == 0:
            self.b.T_A0 = [Tok("a0_%d" % i, alias=self.toks + self.b.T_A0) for i in range(len(self.b.T_A0))]


def inproj_phase(b, l, g, gain, tgain, qg, tqg, kg, tkg):
    c, P = b.c, b.P
    TG, AW, CW = c.TG, c.AW, c.CW
    ntg = TG // 128
    t0 = g * ntg
    norm_to_AT(b, b.xr[g * TG:(g + 1) * TG, :], b.XR[t0:t0 + ntg], ntg, c.D, gain, tgain,
               b.A0, b.T_A0, b.XT, b.HB, b.T_XT, b.T_HB, b.ssb, b.T_ss)
    W = b.prm["w_in"][l]
    mr = MR(b)
    SG = [mr.f32(512, "sg%d" % i) for i in range(TG // 512)]
    st = {"i": 0}

    def obuf():
        i = st["i"]
        st["i"] = (i + 1) % 2
        return b.PTt[i], b.T_PTt[i]

    def qk_epi(scr, T_scr, gcol, tg_):
        def epi(acc, tacc, col, tb, nb):
            h = (col % AW) // 128
            ob, tob = obuf()
            fm_rms_scale(b, acc, tacc, nb, gcol, tg_, 1.0, ob[:, 0:nb], [tob])
            dma(b, "sp", scr[h, :, g * TG + tb:g * TG + tb + nb], ob[:, 0:nb], [tob], [T_scr], T_scr)
        return epi

    gemm(b, b.A0, b.T_A0, c.KC, TG, W, colblocks(0, AW), "FM", qk_epi(b.qT, b.T_qT, qg, tqg), b.Wt, b.T_Wt)
    gemm(b, b.A0, b.T_A0, c.KC, TG, W, colblocks(AW, 2 * AW), "FM", qk_epi(b.kT, b.T_kT, kg, tkg), b.Wt, b.T_Wt)

    def v_epi(acc, tacc, c0, w, tt):
        ob, tob = obuf()
        P.op("act", I("activation", out=ob[:, 0:w], in_=acc, func=AF.Copy), rd=[tacc], wr=[tob])
        dma(b, "sp", b.vS[(t0 + tt) * 128:(t0 + tt + 1) * 128, c0 - 2 * AW:c0 - 2 * AW + w], ob[:, 0:w], [tob], [b.T_vS], b.T_vS)

    gemm(b, b.A0, b.T_A0, c.KC, TG, W, colblocks(2 * AW, 3 * AW), "TM", v_epi, b.Wt, b.T_Wt)
    cols = []
    for cc in range(c.CCH):
        cols += [(3 * AW + CW + cc * 128, 128), (3 * AW + cc * 128, 128)]

    def conv_epi(acc, tacc, col, tb, nb):
        rel = col - 3 * AW
        sg, tsg = SG[tb // 512]
        if rel >= CW:
            P.op("act", I("activation", out=sg[:, 0:nb], in_=acc, func=AF.Sigmoid), rd=[tacc], wr=[tsg])
        else:
            f, tf = ft(b)
            P.op("dve", I("tensor_tensor", out=f[:, 0:nb], in0=acc, in1=sg[:, 0:nb], op=ALU.mult), rd=[tacc, tsg], wr=[tf])
            dma(b, "sp", b.hcS[rel:rel + 128, g * TG + tb:g * TG + tb + nb], f[:, 0:nb], [tf], [b.T_hcS], b.T_hcS)

    gemm(b, b.A0, b.T_A0, c.KC, TG, W, cols, "FM", conv_epi, b.Wt, b.T_Wt)

    def u_epi(acc, tacc, col, tb, nb):
        f, tf = ft(b)
        rel = col - 3 * AW - 2 * CW
        P.op("act", I("activation", out=f[:, 0:nb], in_=acc, func=AF.Copy), rd=[tacc], wr=[tf])
        dma(b, "sp", b.uS[rel:rel + 128, g * TG + tb:g * TG + tb + nb], f[:, 0:nb], [tf], [b.T_uS], b.T_uS)

    gemm(b, b.A0, b.T_A0, c.KC, TG, W, colblocks(3 * AW + 2 * CW, c.INW), "FM", u_epi, b.Wt, b.T_Wt)
    mr.close()


def attn_phase(b, l):
    c, P = b.c, b.P
    NT, NTT, AW = c.NT, c.NTT, c.AW
    mr = MR(b)
    qh, tq = mr.bf16(NT, "qh")
    kh, tk = mr.bf16(NT, "kh")
    vx, tv = mr.bf16(NTT * 130, "vx")
    vx = vx.rearrange("p (t d) -> p t d", d=130)
    mk, tm = mr.bf16(17 * 128, "mask")
    mk3 = mk.rearrange("p (d q) -> p d q", d=17)
    mstg, tms = mr.f32(17 * 128, "mstg")
    dma(b, "sp", mstg.rearrange("p (d q) -> p d q", d=17), b.cst["amask"].rearrange("d k q -> k d q"), [], [tms], tms)
    P.op("dve", I("tensor_copy", out=mk, in_=mstg), rd=[tms], wr=[tm])
    P.op("pool", I("memset", vx[:, :, 128:130], 1.0), wr=[tv])
    ET = [mr.bf16(512, "et%d" % i) for i in range(2)]
    PT = [mr.bf16(512, "ptm%d" % i) for i in range(2)]
    OT = [mr.f32(128, "ot%d" % i) for i in range(2)]
    RD = [mr.f32(1, "rd%d" % i) for i in range(2)]
    ei = 0
    for h in range(c.H):
        dma(b, "sp", qh, b.qT[h], [b.T_qT], [tq], tq)
        dma(b, "sp", kh, b.kT[h], [b.T_kT], [tk], tk)
        dma(b, "sp", vx[:, :, 0:128], b.vS[:, h * 128:(h + 1) * 128].rearrange("(t p) d -> p t d", p=128), [b.T_vS], [tv], tv)
        for qt in range(NTT):
            nd = min(16, qt) + 1
            ob = 4 + (qt % 2)
            oacc, toacc = b.psum[:, ob, 0:129], b.PS[ob]
            for d0 in range(0, nd, 4):
                n = min(4, nd - d0)
                bank = b.gbank
                b.gbank = (b.gbank + 1) % 4
                sc, tsc = b.psum[:, bank, 0:n * 128], b.PS[bank]
                for j in range(n):
                    kt = qt - (d0 + j)
                    P.op("pe", I("matmul", sc[:, j * 128:(j + 1) * 128], lhsT=kh[:, kt * 128:(kt + 1) * 128],
                                 rhs=qh[:, qt * 128:(qt + 1) * 128], start=True, stop=True), rd=[tk, tq], wr=[tsc])
                et, tet = ET[ei % 2]
                pt, tpt = PT[ei % 2]
                ei += 1
                P.op("act", I("activation", out=et[:, 0:n * 128], in_=sc, func=AF.Exp), rd=[tsc], wr=[tet])
                P.op("pool", I("tensor_tensor", out=pt[:, 0:n * 128], in0=et[:, 0:n * 128], in1=mk[:, d0 * 128:(d0 + n) * 128], op=ALU.mult),
                     rd=[tet, tm], wr=[tpt])
                for j in range(n):
                    kt = qt - (d0 + j)
                    P.op("pe", I("matmul", oacc, lhsT=pt[:, j * 128:(j + 1) * 128], rhs=vx[:, kt, 0:129],
                                 start=(d0 + j == 0), stop=(d0 + j == nd - 1)), rd=[tpt, tv], wr=[toacc])
            rd_, trd = RD[qt % 2]
            ot, tot = OT[qt % 2]
            P.op("dve", I("reciprocal", out=rd_, in_=oacc[:, 128:129]), rd=[toacc], wr=[trd])
            P.op("act", I("activation", out=ot, in_=oacc[:, 0:128], func=AF.Copy, scale=rd_), rd=[toacc, trd], wr=[tot])
            dma(b, "sp", b.attS[qt * 128:(qt + 1) * 128, h * 128:(h + 1) * 128], ot, [tot], [b.T_attS], b.T_attS)
    mr.close()


def attn_norm_phase(b, l, gain, tgain):
    c, P = b.c, b.P
    mr = MR(b)
    nk = c.AW // 128
    ST = [mr.bf16(nk * 128, "ast%d" % i) for i in range(2)]
    for t in range(c.NTT):
        stg, tst = ST[t % 2]
        st3 = stg.rearrange("p (k t) -> p k t", k=nk)
        norm_to_AT(b, b.attS[t * 128:(t + 1) * 128, :], [b.T_attS], 1, c.AW, gain, tgain, st3, [tst],
                   b.XT, b.HB, b.T_XT, b.T_HB, b.ssb, b.T_ss)
        dma(b, "sp", b.mixT[0:c.AW, t * 128:(t + 1) * 128].rearrange("(k p) t -> p k t", p=128), st3, [tst], [b.T_mixT], b.T_mixT)
    mr.close()


def wout_phase(b, l, g):
    c = b.c
    TG = c.TG
    ntg = TG // 128
    for t in range(ntg):
        dma(b, "sp", b.A0[:, :, t * 128:(t + 1) * 128],
            b.mixT[:, g * TG + t * 128:g * TG + (t + 1) * 128].rearrange("(k p) t -> p k t", p=128), [b.T_mixT], [b.T_A0[t]], b.T_A0[t])
    gemm(b, b.A0, b.T_A0, c.KC, TG, b.prm["w_out"][l], colblocks(0, c.D), "TM", rmw_epi(b, g * ntg), b.Wt, b.T_Wt)


def group_rms_store(b, Ys, S3, tS3, R, tR, width, gain, gcol0, tgain, row0, t_0, nb):
    P = b.P
    P.op("act", I("activation", out=R[:, 0:nb], in_=S3, func=AF.Sqrt, scale=1.0 / width, bias=1e-6), rd=[tS3], wr=[tR])
    P.op("dve", I("reciprocal", out=R[:, 0:nb], in_=R[:, 0:nb]), rd=[tR], wr=[tR])
    for cc, (y, ty) in enumerate(Ys):
        f, tf = ft(b)
        i = cc % 2
        ob, tob = b.PTt[i], b.T_PTt[i]
        P.op("dve", I("tensor_tensor", out=f[:, 0:nb], in0=y, in1=R[:, 0:nb], op=ALU.mult), rd=[ty, tR], wr=[tf])
        P.op("act", I("activation", out=ob[:, 0:nb], in_=f[:, 0:nb], func=AF.Copy, scale=gain[:, gcol0 + cc:gcol0 + cc + 1]),
             rd=[tf, tgain], wr=[tob])
        dma(b, "sp", b.mixT[row0 + cc * 128:row0 + (cc + 1) * 128, t_0:t_0 + nb], ob[:, 0:nb], [tob], [b.T_mixT], b.T_mixT)


def sumsq_acc(b, y, ty, S, tS, first, last, nb):
    P = b.P
    sq, tsq = ft(b)
    P.op("pool", I("tensor_tensor", out=sq[:, 0:nb], in0=y, in1=y, op=ALU.mult), rd=[ty], wr=[tsq])
    P.op("pe", I("matmul", S, lhsT=b.ones_f, rhs=sq[:, 0:nb], start=first, stop=last), rd=[tsq, b.T_ones], wr=[tS])


def conv_phase(b, l, go, tgo):
    c, P = b.c, b.P
    NT, CW, CCH, AW = c.NT, c.CW, c.CCH, c.AW
    mr = MR(b)
    HX = [mr.f32(544, "hx%d" % i) for i in range(2)]
    Y = [mr.f32(512, "cy%d" % i) for i in range(CCH)]
    M, tM = mr.f32(512, "cm")
    R, tR = mr.f32(512, "cr")
    dw = [load_mat_pm(b, "dw%d" % cc, b.prm["conv_dw"][l][:, cc * 128:(cc + 1) * 128], 31) for cc in range(CCH)]
    cb, tcb = load_vec(b, "c_b", b.prm["conv_b"][l], CW)
    lg, tlg = load_vec(b, "c_lg", b.prm["conv_ln_g"][l], CW)
    lb, tlb = load_vec(b, "c_lb", b.prm["conv_ln_b"][l], CW)
    for tbk in range(NT // 512):
        t_0 = tbk * 512
        S1, tS1 = b.psum[:, 6, 0:512], b.PS[6]
        S2, tS2 = b.psum[:, 7, 0:512], b.PS[7]
        for cc in range(CCH):
            hx, thx = HX[cc % 2]
            rows = b.hcS[cc * 128:(cc + 1) * 128, :]
            if tbk == 0:
                P.op("pool", I("memset", hx[:, 0:30], 0.0), wr=[thx])
                dma(b, "sp", hx[:, 30:542], rows[:, 0:512], [b.T_hcS], [thx], thx)
            else:
                dma(b, "sp", hx[:, 0:542], rows[:, t_0 - 30:t_0 + 512], [b.T_hcS], [thx], thx)
            y, ty = Y[cc]
            d, td = dw[cc]
            P.op("dve", I("tensor_scalar", out=y, in0=hx[:, 0:512], scalar1=d[:, 0:1], scalar2=cb[:, cc:cc + 1],
                          op0=ALU.mult, op1=ALU.add), rd=[thx, td, tcb], wr=[ty])
            for j in range(1, 31):
                P.op("dve", I("scalar_tensor_tensor", out=y, in0=hx[:, j:j + 512], scalar=d[:, j:j + 1], in1=y,
                              op0=ALU.mult, op1=ALU.add), rd=[thx, td, ty], wr=[ty])
            P.op("pe", I("matmul", S1, lhsT=b.ones_f, rhs=y, start=(cc == 0), stop=(cc == CCH - 1)), rd=[ty, b.T_ones], wr=[tS1])
            sumsq_acc(b, y, ty, S2, tS2, cc == 0, cc == CCH - 1, 512)
        msq, tmsq = ft(b)
        var, tvar = ft(b)
        P.op("act", I("activation", out=M, in_=S1, func=AF.Copy, scale=1.0 / CW), rd=[tS1], wr=[tM])
        P.op("pool", I("tensor_tensor", out=msq, in0=M, in1=M, op=ALU.mult), rd=[tM], wr=[tmsq])
        P.op("dve", I("scalar_tensor_tensor", out=var, in0=S2, scalar=1.0 / CW, in1=msq, op0=ALU.mult, op1=ALU.subtract),
             rd=[tS2, tmsq], wr=[tvar])
        P.op("act", I("activation", out=R, in_=var, func=AF.Sqrt, bias=1e-6), rd=[tvar], wr=[tR])
        P.op("dve", I("reciprocal", out=R, in_=R), rd=[tR], wr=[tR])
        S3, tS3 = b.psum[:, 6, 0:512], b.PS[6]
        for cc in range(CCH):
            y, ty = Y[cc]
            sg, tsg = ft(b)
            P.op("dve", I("tensor_tensor", out=y, in0=y, in1=M, op=ALU.subtract), rd=[ty, tM], wr=[ty])
            P.op("dve", I("tensor_tensor", out=y, in0=y, in1=R, op=ALU.mult), rd=[ty, tR], wr=[ty])
            P.op("act", I("activation", out=sg, in_=y, func=AF.Sigmoid, scale=lg[:, cc:cc + 1], bias=lb[:, cc:cc + 1]),
                 rd=[ty, tlg, tlb], wr=[tsg])
            P.op("dve", I("tensor_scalar", out=y, in0=y, scalar1=lg[:, cc:cc + 1], scalar2=lb[:, cc:cc + 1], op0=ALU.mult, op1=ALU.add),
                 rd=[ty, tlg, tlb, tsg], wr=[ty])
            P.op("pool", I("tensor_tensor", out=y, in0=y, in1=sg, op=ALU.mult), rd=[ty, tsg], wr=[ty])
            sumsq_acc(b, y, ty, S3, tS3, cc == 0, cc == CCH - 1, 512)
        group_rms_store(b, Y, S3, tS3, R, tR, CW, go, AW // 128, tgo, AW, t_0, 512)
    mr.close()


def nbank(b):
    k = b.gbank
    b.gbank = (k + 1) % 4
    return b.psum[:, k, 0:512], b.PS[k]


def ssm_phase(b, l, go, tgo):
    c, P = b.c, b.P
    NT, SW, SCH, AW, CW = c.NT, c.SW, c.SCH, c.AW, c.CW
    G = SW // 16
    LOG = int(math.log2(NT))
    CH = min(1024, NT)
    mr = MR(b, 0)
    prm = b.prm

    def load_dup(name, src):
        stg, ts = mr.f32(128, name + "s")
        dst, td = mr.f32(G, name)
        dma(b, "sp", stg[0:G, 0:64], src, [], [ts], ts)
        dma(b, "sp", stg[0:G, 64:128], src, [], [ts], ts)
        P.op("pe", I("transpose", out=b.psum[:, 7, 0:G], in_=stg[0:G, :], identity=b.ident_f[0:G, 0:G]), rd=[ts, b.T_ident_f], wr=[b.PS[7]])
        P.op("dve", I("tensor_copy", out=dst, in_=b.psum[:, 7, 0:G]), rd=[b.PS[7]], wr=[td])
        return dst, td

    are, tare = load_dup("are", prm["ssm_a_re"][l])
    aim, taim = load_dup("aim", prm["ssm_a_im"][l])
    stp, tstp = mr.f32(G, "stp")
    dma(b, "sp", stp.rearrange("p (o g) -> p o g", o=1), prm["ssm_log_step"][l:l + 1, :].partition_broadcast(128), [], [tstp], tstp)
    P.op("act", I("activation", out=stp, in_=stp, func=AF.Exp), rd=[tstp], wr=[tstp])
    TH, tTH = mr.f32(G, "th")
    RHO, tRHO = mr.f32(G, "rho")
    P.op("dve", I("tensor_tensor", out=TH, in0=aim, in1=stp, op=ALU.mult), rd=[taim, tstp], wr=[tTH])
    P.op("dve", I("tensor_tensor", out=RHO, in0=are, in1=stp, op=ALU.mult), rd=[tare, tstp], wr=[tRHO])
    P.op("act", I("activation", out=RHO, in_=RHO, func=AF.Exp), rd=[tRHO], wr=[tRHO])
    NL = 4 + LOG
    CA, tCA = mr.f32(NL * G, "ca")
    SA, tSA = mr.f32(NL * G, "sa")
    CA3 = CA.rearrange("p (k g) -> p k g", g=G)
    SA3 = SA.rearrange("p (k g) -> p k g", g=G)
    hp, thp = palloc(b, "halfpi", 1)
    P.op("pool", I("memset", hp, math.pi / 2), wr=[thp])
    P.op("act", I("activation", out=SA3[:, 0, :], in_=TH, func=AF.Sin, scale=1.0 / 16), rd=[tTH], wr=[tSA])
    P.op("act", I("activation", out=CA3[:, 0, :], in_=TH, func=AF.Sin, scale=1.0 / 16, bias=hp[:, 0:1]), rd=[tTH, thp], wr=[tCA])
    t1, tt1 = mr.f32(G, "t1")
    t2, tt2 = mr.f32(G, "t2")
    for i in range(NL - 1):
        P.op("dve", I("tensor_tensor", out=t1, in0=CA3[:, i, :], in1=CA3[:, i, :], op=ALU.mult), rd=[tCA], wr=[tt1])
        P.op("pool", I("tensor_tensor", out=t2, in0=SA3[:, i, :], in1=SA3[:, i, :], op=ALU.mult), rd=[tSA], wr=[tt2])
        P.op("dve", I("tensor_tensor", out=CA3[:, i + 1, :], in0=t1, in1=t2, op=ALU.subtract), rd=[tt1, tt2], wr=[tCA])
        P.op("dve", I("scalar_tensor_tensor", out=SA3[:, i + 1, :], in0=SA3[:, i, :], scalar=2.0, in1=CA3[:, i, :],
                      op0=ALU.mult, op1=ALU.mult), rd=[tSA, tCA], wr=[tSA])
    KR, tKR = mr.f32(G, "kr")
    KI, tKI = mr.f32(G, "ki")
    KIN, tKIN = mr.f32(G, "kin")
    KRN, tKRN = mr.f32(G, "krn")
    nr, tnr = mr.f32(G, "nr")
    ni, tni = mr.f32(G, "ni")
    dn, tdn = mr.f32(G, "dn")
    P.op("dve", I("tensor_tensor", out=nr, in0=RHO, in1=CA3[:, 4, :], op=ALU.mult), rd=[tRHO, tCA], wr=[tnr])
    P.op("dve", I("tensor_scalar", out=nr, in0=nr, scalar1=-1.0, scalar2=None, op0=ALU.add), rd=[tnr], wr=[tnr])
    P.op("dve", I("tensor_tensor", out=ni, in0=RHO, in1=SA3[:, 4, :], op=ALU.mult), rd=[tRHO, tSA], wr=[tni])
    P.op("dve", I("tensor_tensor", out=dn, in0=are, in1=are, op=ALU.mult), rd=[tare], wr=[tdn])
    P.op("dve", I("tensor_tensor", out=t1, in0=aim, in1=aim, op=ALU.mult), rd=[taim], wr=[tt1])
    P.op("dve", I("tensor_tensor", out=dn, in0=dn, in1=t1, op=ALU.add), rd=[tdn, tt1], wr=[tdn])
    P.op("dve", I("reciprocal", out=dn, in_=dn), rd=[tdn], wr=[tdn])
    P.op("dve", I("tensor_tensor", out=KR, in0=nr, in1=are, op=ALU.mult), rd=[tnr, tare], wr=[tKR])
    P.op("dve", I("tensor_tensor", out=t1, in0=ni, in1=aim, op=ALU.mult), rd=[tni, taim], wr=[tt1])
    P.op("dve", I("tensor_tensor", out=KR, in0=KR, in1=t1, op=ALU.add), rd=[tKR, tt1], wr=[tKR])
    P.op("dve", I("tensor_tensor", out=KR, in0=KR, in1=dn, op=ALU.mult), rd=[tKR, tdn], wr=[tKR])
    P.op("dve", I("tensor_tensor", out=KI, in0=ni, in1=are, op=ALU.mult), rd=[tni, tare], wr=[tKI])
    P.op("dve", I("tensor_tensor", out=t1, in0=nr, in1=aim, op=ALU.mult), rd=[tnr, taim], wr=[tt1])
    P.op("dve", I("tensor_tensor", out=KI, in0=KI, in1=t1, op=ALU.subtract), rd=[tKI, tt1], wr=[tKI])
    P.op("dve", I("tensor_tensor", out=KI, in0=KI, in1=dn, op=ALU.mult), rd=[tKI, tdn], wr=[tKI])
    P.op("dve", I("tensor_scalar", out=KIN[0:64, :], in0=KI[0:64, :], scalar1=-1.0, scalar2=None, op0=ALU.mult), rd=[tKI], wr=[tKIN])
    P.op("dve", I("tensor_copy", out=KIN[64:128, :], in_=KI[64:128, :]), rd=[tKI], wr=[tKIN])
    P.op("dve", I("tensor_copy", out=KRN[0:64, :], in_=KR[0:64, :]), rd=[tKR], wr=[tKRN])
    P.op("dve", I("tensor_scalar", out=KRN[64:128, :], in0=KR[64:128, :], scalar1=-1.0, scalar2=None, op0=ALU.mult), rd=[tKR], wr=[tKRN])
    dsk, tdsk = load_vec(b, "s_d", prm["ssm_d"][l], SW)
    CT, tCT = mr.f32(NT, "ct")
    ST, tST = mr.f32(NT, "st")
    WW, tWW = mr.f32(2 * CH, "ww")
    WIN, WO = WW[:, 0:CH], WW[:, CH:2 * CH]
    tWIN, tWO = Tok("win", alias=[tWW]), Tok("wo", alias=[tWW])
    mr.toks += [tWIN, tWO]
    P1, tP1 = mr.bf16(CH, "p1")
    P2, tP2 = mr.bf16(CH, "p2")
    UB, tUB = mr.bf16(NT, "ub")
    RHOB, tRHOB = mr.f32(CH, "rhob")
    ZA, tZA = mr.f32(128, "za")
    ZB, tZB = mr.f32(128, "zb")
    X1, tX1 = mr.f32(16, "x1")
    X2, tX2 = mr.f32(16, "x2")
    CC1, tCC1 = mr.f32(128, "cc1")
    CC2, tCC2 = mr.f32(128, "cc2")
    LA, tLA = mr.bf16(128, "la")
    LB, tLB = mr.bf16(128, "lb")
    L1, tL1 = mr.bf16(128, "l1")
    L2, tL2 = mr.bf16(128, "l2")
    CARRY, tCARRY = mr.f32(1, "carry")
    YA, tYA = b.XT[:, 0:NT], b.T_XT
    for sg in range(SCH):
        ch0 = sg * 128
        dma(b, "sp", YA, b.uS[ch0:ch0 + 128, :], [b.T_uS], [tYA], tYA)
        P.op("act", I("activation", out=UB, in_=YA, func=AF.Copy), rd=[tYA], wr=[tUB])
        P.op("dve", I("tensor_scalar", out=YA, in0=YA, scalar1=dsk[:, sg:sg + 1], scalar2=None, op0=ALU.mult), rd=[tYA, tdsk, tUB], wr=[tYA])
        for gi in range(8):
            g = sg * 8 + gi
            P.op("pool", I("memset", CT[:, 0:1], 1.0), wr=[tCT])
            P.op("pool", I("memset", ST[:, 0:1], 0.0), wr=[tST])
            n = 1
            for k in range(LOG):
                ck, sk = CA3[:, 4 + k, g:g + 1], SA3[:, 4 + k, g:g + 1]
                tmp = WW[:, 0:n]
                P.op("dve", I("tensor_scalar", out=tmp, in0=ST[:, 0:n], scalar1=sk, scalar2=None, op0=ALU.mult), rd=[tST, tSA], wr=[tWW, tWIN, tWO])
                P.op("dve", I("scalar_tensor_tensor", out=CT[:, n:2 * n], in0=CT[:, 0:n], scalar=ck, in1=tmp, op0=ALU.mult, op1=ALU.subtract),
                     rd=[tCT, tCA, tWW], wr=[tCT])
                P.op("dve", I("tensor_scalar", out=tmp, in0=CT[:, 0:n], scalar1=sk, scalar2=None, op0=ALU.mult), rd=[tCT, tSA], wr=[tWW])
                P.op("dve", I("scalar_tensor_tensor", out=ST[:, n:2 * n], in0=ST[:, 0:n], scalar=ck, in1=tmp, op0=ALU.mult, op1=ALU.add),
                     rd=[tST, tCA, tWW], wr=[tST])
                n *= 2
            tWIN.lw, tWIN.rd, tWO.lw, tWO.rd = dict(tWW.lw), dict(tWW.rd), dict(tWW.lw), dict(tWW.rd)
            dma(b, "sp", X1[0:64, :], prm["ssm_b_re"][l][g], [], [tX1], tX1)
            dma(b, "sp", X1[64:128, :], prm["ssm_b_im"][l][g], [], [tX1], tX1)
            dma(b, "sp", X2[0:64, :], prm["ssm_b_im"][l][g], [], [tX2], tX2)
            dma(b, "sp", X2[64:128, :], prm["ssm_b_re"][l][g], [], [tX2], tX2)
            P.op("pool", I("memset", ZA, 0.0), wr=[tZA])
            P.op("pool", I("memset", ZB, 0.0), wr=[tZB])
            zs = slice(gi * 16, gi * 16 + 16)
            P.op("dve", I("tensor_scalar", out=ZA[:, zs], in0=X1, scalar1=KR[:, g:g + 1], scalar2=None, op0=ALU.mult), rd=[tX1, tKR], wr=[tZA])
            P.op("dve", I("scalar_tensor_tensor", out=ZA[:, zs], in0=X2, scalar=KIN[:, g:g + 1], in1=ZA[:, zs], op0=ALU.mult, op1=ALU.add),
                 rd=[tX2, tKIN, tZA], wr=[tZA])
            P.op("dve", I("tensor_scalar", out=ZB[:, zs], in0=X2, scalar1=KRN[:, g:g + 1], scalar2=None, op0=ALU.mult), rd=[tX2, tKRN], wr=[tZB])
            P.op("dve", I("scalar_tensor_tensor", out=ZB[:, zs], in0=X1, scalar=KI[:, g:g + 1], in1=ZB[:, zs], op0=ALU.mult, op1=ALU.add),
                 rd=[tX1, tKI, tZB], wr=[tZB])
            for Z, tZ, L, tL, bank in ((ZA, tZA, LA, tLA, 6), (ZB, tZB, LB, tLB, 7)):
                pz = b.psum[:, bank, 0:128]
                P.op("pe", I("transpose", out=pz, in_=Z, identity=b.ident_f), rd=[tZ, b.T_ident_f], wr=[b.PS[bank]])
                P.op("act", I("activation", out=L, in_=pz, func=AF.Copy), rd=[b.PS[bank]], wr=[tL])
            dma(b, "sp", CC1[0:16, 0:64], prm["ssm_c_re"][l][g], [], [tCC1], tCC1)
            dma(b, "sp", CC1[0:16, 64:128], prm["ssm_c_im"][l][g], [], [tCC1], tCC1)
            dma(b, "sp", CC2[0:16, 0:64], prm["ssm_c_im"][l][g], [], [tCC2], tCC2)
            dma(b, "sp", CC2[0:16, 64:128], prm["ssm_c_re"][l][g], [], [tCC2], tCC2)
            P.op("dve", I("tensor_scalar", out=CC1[0:16, 64:128], in0=CC1[0:16, 64:128], scalar1=-1.0, scalar2=None, op0=ALU.mult), rd=[tCC1], wr=[tCC1])
            P.op("dve", I("tensor_scalar", out=CC2[0:16, :], in0=CC2[0:16, :], scalar1=-1.0, scalar2=None, op0=ALU.mult), rd=[tCC2], wr=[tCC2])
            for CCx, tCx, L, tL, bank in ((CC1, tCC1, L1, tL1, 6), (CC2, tCC2, L2, tL2, 7)):
                pz = b.psum[:, bank, 0:16]
                P.op("pe", I("transpose", out=pz, in_=CCx[0:16, :], identity=b.ident_f[0:16, 0:16]), rd=[tCx, b.T_ident_f], wr=[b.PS[bank]])
                P.op("pool", I("memset", L, 0.0), wr=[tL])
                P.op("act", I("activation", out=L[:, zs], in_=pz, func=AF.Copy), rd=[b.PS[bank]], wr=[tL])
            P.op("act", I("activation", out=RHOB, in_=RHO[:, g:g + 1].to_broadcast([128, CH]), func=AF.Copy), rd=[tRHO], wr=[tRHOB])
            for ci in range(NT // CH):
                for blk in range(CH // 512):
                    t_0 = ci * CH + blk * 512
                    pA, tpA = nbank(b)
                    pB, tpB = nbank(b)
                    P.op("pe", I("matmul", pA, lhsT=LA, rhs=UB[:, t_0:t_0 + 512], start=True, stop=True), rd=[tLA, tUB], wr=[tpA])
                    P.op("pe", I("matmul", pB, lhsT=LB, rhs=UB[:, t_0:t_0 + 512], start=True, stop=True), rd=[tLB, tUB], wr=[tpB])
                    f1, tf1 = ft(b)
                    f2, tf2 = ft(b)
                    P.op("dve", I("tensor_tensor", out=f1, in0=pA, in1=CT[:, t_0:t_0 + 512], op=ALU.mult), rd=[tpA, tCT], wr=[tf1])
                    P.op("dve", I("tensor_tensor", out=f2, in0=pB, in1=ST[:, t_0:t_0 + 512], op=ALU.mult), rd=[tpB, tST], wr=[tf2])
                    P.op("pool", I("tensor_tensor", out=WIN[:, blk * 512:(blk + 1) * 512], in0=f1, in1=f2, op=ALU.add), rd=[tf1, tf2], wr=[tWIN])
                if ci == 0:
                    P.op("dve", I("tensor_tensor_scan", out=WO, data0=RHOB, data1=WIN, initial=0.0, op0=ALU.mult, op1=ALU.add),
                         rd=[tRHOB, tWIN], wr=[tWO])
                else:
                    P.op("dve", I("tensor_tensor_scan", out=WO, data0=RHOB, data1=WIN, initial=CARRY[:, 0:1], op0=ALU.mult, op1=ALU.add),
                         rd=[tRHOB, tWIN, tCARRY], wr=[tWO])
                P.op("act", I("activation", out=CARRY, in_=WO[:, CH - 1:CH], func=AF.Copy), rd=[tWO], wr=[tCARRY])
                P.op("pool", I("tensor_tensor", out=P1, in0=WO, in1=CT[:, ci * CH:(ci + 1) * CH], op=ALU.mult), rd=[tWO, tCT], wr=[tP1])
                P.op("dve", I("tensor_tensor", out=P2, in0=WO, in1=ST[:, ci * CH:(ci + 1) * CH], op=ALU.mult), rd=[tWO, tST], wr=[tP2])
                for blk in range(CH // 512):
                    t_0 = ci * CH + blk * 512
                    pY, tpY = nbank(b)
                    P.op("pe", I("matmul", pY, lhsT=L1, rhs=P1[:, blk * 512:(blk + 1) * 512], start=True, stop=False), rd=[tL1, tP1], wr=[tpY])
                    P.op("pe", I("matmul", pY, lhsT=L2, rhs=P2[:, blk * 512:(blk + 1) * 512], start=False, stop=True), rd=[tL2, tP2], wr=[tpY])
                    P.op("dve", I("tensor_tensor", out=YA[:, t_0:t_0 + 512], in0=pY, in1=YA[:, t_0:t_0 + 512], op=ALU.add), rd=[tpY, tYA], wr=[tYA])
        for t_0 in range(0, NT, 512):
            y = YA[:, t_0:t_0 + 512]
            a, ta = ft(b)
            s_, ts_ = ft(b)
            P.op("pool", I("tensor_tensor", out=a, in0=y, in1=y, op=ALU.mult), rd=[tYA], wr=[ta])
            P.op("dve", I("tensor_scalar", out=a, in0=a, scalar1=0.044715, scalar2=1.0, op0=ALU.mult, op1=ALU.add), rd=[ta], wr=[ta])
            P.op("pool", I("tensor_tensor", out=a, in0=a, in1=y, op=ALU.mult), rd=[ta, tYA], wr=[ta])
            P.op("act", I("activation", out=s_, in_=a, func=AF.Sigmoid, scale=1.5957691216), rd=[ta], wr=[ts_])
            P.op("dve", I("tensor_tensor", out=a, in0=y, in1=s_, op=ALU.mult), rd=[tYA, ts_, ta], wr=[ta])
            dma(b, "sp", b.zS[ch0:ch0 + 128, t_0:t_0 + 512], a, [ta], [b.T_zS], b.T_zS)
    mr.close()
    mr = MR(b)
    WG, tWG = mr.bf16(SCH * SW, "wg")
    WG3 = WG.rearrange("p (k n) -> p k n", k=SCH)
    dma(b, "pool", WG3, prm["ssm_w_glu"][l].rearrange("(k p) n -> p k n", p=128), [], [tWG], tWG)
    ZF, tZFl = mr.f32(SCH * 512, "zf")
    ZF3 = ZF.rearrange("p (k t) -> p k t", k=SCH)
    tZF = [Tok("zf%d" % i, alias=[tZFl]) for i in range(SCH)]
    mr.toks += tZF
    ZBf, tZB_ = mr.bf16(SCH * 512, "zbf")
    ZB3 = ZBf.rearrange("p (k t) -> p k t", k=SCH)
    R, tR = mr.f32(512, "gr")
    bg, tbg = load_vec(b, "s_bg", prm["ssm_b_glu"][l], SW)
    for t_0 in range(0, NT, 512):
        dma(b, "sp", ZF3, b.zS[:, t_0:t_0 + 512].rearrange("(k p) t -> p k t", p=128), [b.T_zS], tZF + [tZFl], tZFl)
        P.op("act", I("activation", out=ZBf, in_=ZF, func=AF.Copy), rd=tZF, wr=[tZB_])
        S3, tS3 = b.psum[:, 6, 0:512], b.PS[6]
        for oc in range(SCH):
            acc, tacc = nbank(b)
            for kc in range(SCH):
                P.op("pe", I("matmul", acc, lhsT=WG3[:, kc, oc * 128:(oc + 1) * 128], rhs=ZB3[:, kc, :], start=(kc == 0), stop=(kc == SCH - 1)),
                     rd=[tWG, tZB_], wr=[tacc])
            sg_, tsg_ = ft(b)
            P.op("act", I("activation", out=sg_, in_=acc, func=AF.Sigmoid, bias=bg[:, oc:oc + 1]), rd=[tacc, tbg], wr=[tsg_])
            P.op("pool", I("tensor_tensor", out=ZF3[:, oc, :], in0=ZF3[:, oc, :], in1=sg_, op=ALU.mult), rd=[tZF[oc], tsg_], wr=[tZF[oc]])
            sumsq_acc(b, ZF3[:, oc, :], tZF[oc], S3, tS3, oc == 0, oc == SCH - 1, 512)
        group_rms_store(b, [(ZF3[:, oc, :], tZF[oc]) for oc in range(SCH)], S3, tS3, R, tR, SW, go, (AW + CW) // 128, tgo, AW + CW, t_0, 512)
    mr.close()
```
